# Optimizing a Trainium2 kernel written in Bass

```python
import math
import jax, jax.numpy as jnp
from jax import lax
import numpy as np

D_MODEL = 2048
BATCH = 1
SEQ = 16384
DEPTH = 4
DEC_BATCH = 16
DEC_SEQ = 32
PAST_LEN = 1024

CHUNK = 64
N_MIXERS = 2
N_GDN = (DEPTH + 1) // 2
N_ATT = DEPTH // 2
EPS = 1e-6
GDN_QK_HEADS = 16
GDN_V_HEADS = 32
GDN_DK = 128
GDN_DV = 128
CONV_W = 4
GDN_QKV = 2 * GDN_QK_HEADS * GDN_DK + GDN_V_HEADS * GDN_DV
GDN_Z = GDN_V_HEADS * GDN_DV
GDN_PROJ = GDN_QKV + GDN_Z + 2 * GDN_V_HEADS
ATT_HEADS = 16
ATT_DH = D_MODEL // ATT_HEADS
LEFT_CHUNKS = 8
BAND_PAST = LEFT_CHUNKS * CHUNK
BAND = BAND_PAST + CHUNK
MAX_REL = 256
D_FF = -(-8 * D_MODEL // (3 * 256)) * 256

kernel_name = 'hybrid_gdn_chunkband_stream_step'


def rmsnorm(x, w):
    xf = x.astype(jnp.float32)
    y = xf * lax.rsqrt(jnp.mean(xf * xf, axis=-1, keepdims=True) + EPS)
    return (y * w.astype(jnp.float32)).astype(x.dtype)


def l2norm(x):
    xf = x.astype(jnp.float32)
    return xf * lax.rsqrt(jnp.sum(xf * xf, axis=-1, keepdims=True) + EPS)


def swiglu(h, w_gate, w_up, w_down):
    return (jax.nn.silu(h @ w_gate) * (h @ w_up)) @ w_down


def gated_delta_chunked(q, k, v, g, beta, s0, chunk):
    b, l, h, dk = q.shape
    dv = v.shape[-1]
    n = l // chunk

    def blocks(t):
        t = t.reshape((b, n, chunk) + t.shape[2:])
        return jnp.swapaxes(jnp.swapaxes(t, 0, 1), 2, 3)

    idx = jnp.arange(chunk)
    incl = idx[:, None] >= idx[None, :]
    strict = idx[:, None] > idx[None, :]
    eye = jnp.eye(chunk, dtype=jnp.float32)

    def step(s, blk):
        qc, kc, vc, gc, bc = blk
        gcum = jnp.cumsum(gc, axis=-1)
        decay = jnp.exp(jnp.where(incl, gcum[..., :, None] - gcum[..., None, :], -jnp.inf))
        kk = jnp.einsum('bhid,bhjd->bhij', kc, kc)
        a_mat = eye + jnp.where(strict, kk * decay * bc[..., :, None], 0.0)
        rhs = jnp.concatenate([vc * bc[..., None], kc * (bc * jnp.exp(gcum))[..., None]], axis=-1)
        sol = lax.linalg.triangular_solve(a_mat, rhs, left_side=True, lower=True, unit_diagonal=True)
        u, w = sol[..., :dv], sol[..., dv:]
        v_new = u - jnp.einsum('bhck,bhkv->bhcv', w, s)
        qk = jnp.einsum('bhid,bhjd->bhij', qc, kc) * decay
        o = (jnp.einsum('bhck,bhkv->bhcv', qc * jnp.exp(gcum)[..., None], s)
             + jnp.einsum('bhij,bhjv->bhiv', qk, v_new))
        g_last = gcum[..., -1:]
        s_new = (s * jnp.exp(g_last)[..., None]
                 + jnp.einsum('bhck,bhcv->bhkv', kc * jnp.exp(g_last - gcum)[..., None], v_new))
        return s_new, o

    s, o = lax.scan(step, s0, (blocks(q), blocks(k), blocks(v), blocks(g), blocks(beta)))
    o = jnp.swapaxes(jnp.swapaxes(o, 2, 3), 0, 1).reshape(b, l, h, dv)
    return o, s


def gdn_mixer(h, conv_prev, s0, chunk, w_in, w_conv, a_log, dt_bias, w_norm, w_out):
    b, l, _ = h.shape
    proj = h @ w_in
    qkv = proj[..., :GDN_QKV]
    z = proj[..., GDN_QKV:GDN_QKV + GDN_Z]
    beta_in = proj[..., GDN_QKV + GDN_Z:GDN_QKV + GDN_Z + GDN_V_HEADS]
    a_in = proj[..., GDN_QKV + GDN_Z + GDN_V_HEADS:]
    xc = jnp.concatenate([conv_prev.astype(qkv.dtype), qkv], axis=1)
    conv = xc[:, 0:l] * w_conv[0]
    for j in range(1, CONV_W):
        conv = conv + xc[:, j:j + l] * w_conv[j]
    conv = jax.nn.silu(conv)
    new_conv = xc[:, l:]
    nqk = GDN_QK_HEADS * GDN_DK
    rep = GDN_V_HEADS // GDN_QK_HEADS
    q = conv[..., :nqk].reshape(b, l, GDN_QK_HEADS, GDN_DK)
    k = conv[..., nqk:2 * nqk].reshape(b, l, GDN_QK_HEADS, GDN_DK)
    v = conv[..., 2 * nqk:].reshape(b, l, GDN_V_HEADS, GDN_DV).astype(jnp.float32)
    q = jnp.repeat(l2norm(q) * GDN_DK ** -0.5, rep, axis=2)
    k = jnp.repeat(l2norm(k), rep, axis=2)
    beta = jax.nn.sigmoid(beta_in.astype(jnp.float32))
    g = -jnp.exp(a_log.astype(jnp.float32)) * jax.nn.softplus(a_in.astype(jnp.float32) + dt_bias.astype(jnp.float32))
    o, s = gated_delta_chunked(q, k, v, g, beta, s0.astype(jnp.float32), chunk)
    o = o * lax.rsqrt(jnp.mean(o * o, axis=-1, keepdims=True) + EPS) * w_norm.astype(jnp.float32)
    o = o * jax.nn.silu(z.reshape(b, l, GDN_V_HEADS, GDN_DV).astype(jnp.float32))
    y = o.reshape(b, l, GDN_Z).astype(h.dtype) @ w_out
    return y, new_conv, s


def rel_bias(table, q_pos, k_pos):
    rel = jnp.clip(q_pos[:, None] - k_pos[None, :], -MAX_REL, MAX_REL) + MAX_REL
    return table.astype(jnp.float32)[:, rel]


def attend(q, k, v, bias):
    s = jnp.einsum('bqhd,bkhd->bhqk', q, k).astype(jnp.float32) * ATT_DH ** -0.5 + bias
    p = jax.nn.softmax(s, axis=-1).astype(v.dtype)
    return jnp.einsum('bhqk,bkhd->bqhd', p, v)


def att_project(h, w_qkv, b_qkv):
    b, l, _ = h.shape
    q, k, v = jnp.split(h @ w_qkv + b_qkv, 3, axis=-1)
    shape = (b, l, ATT_HEADS, ATT_DH)
    return q.reshape(shape), k.reshape(shape), v.reshape(shape)


def att_prompt(h, w_qkv, b_qkv, table, w_o, b_o):
    b, l, _ = h.shape
    q, k, v = att_project(h, w_qkv, b_qkv)
    nc = l // CHUNK
    pad = ((0, 0), (BAND_PAST, 0), (0, 0), (0, 0))
    kp, vp = jnp.pad(k, pad), jnp.pad(v, pad)
    qc = jnp.swapaxes(q.reshape(b, nc, CHUNK, ATT_HEADS, ATT_DH), 0, 1)
    offs = jnp.arange(BAND)
    bias = rel_bias(table, jnp.arange(CHUNK) + BAND_PAST, offs)

    def one_chunk(args):
        c, q_blk = args
        start = c * CHUNK
        k_blk = lax.dynamic_slice_in_dim(kp, start, BAND, axis=1)
        v_blk = lax.dynamic_slice_in_dim(vp, start, BAND, axis=1)
        valid = start + offs >= BAND_PAST
        return attend(q_blk, k_blk, v_blk, jnp.where(valid, bias, -jnp.inf))

    o = lax.map(one_chunk, (jnp.arange(nc), qc))
    o = jnp.swapaxes(o, 0, 1).reshape(b, l, D_MODEL)
    keep = min(BAND_PAST, l)
    return o @ w_o + b_o, k[:, l - keep:], v[:, l - keep:]


def att_sample(h, cache_k, cache_v, w_qkv, b_qkv, table, w_o, b_o):
    b, l, _ = h.shape
    r = cache_k.shape[1]
    q, k, v = att_project(h, w_qkv, b_qkv)
    k_all = jnp.concatenate([cache_k.astype(k.dtype), k], axis=1)
    v_all = jnp.concatenate([cache_v.astype(v.dtype), v], axis=1)
    bias = rel_bias(table, jnp.arange(l) + r, jnp.arange(r + l))
    o = attend(q, k_all, v_all, bias).reshape(b, l, D_MODEL)
    return o @ w_o + b_o, k, v


def setup_inputs(seed: int = 0) -> dict:
    key = jax.random.key(seed)
    ks = jax.random.split(key, 24)
    f32 = jnp.float32

    def nrm(k, shape, scale):
        return jax.random.normal(k, shape, f32) * scale

    rows = min(BAND_PAST, PAST_LEN)
    dt = jnp.exp(jax.random.uniform(ks[10], (N_GDN, GDN_V_HEADS), f32, math.log(1e-3), math.log(1e-1)))
    return {
        'x_prompt': nrm(ks[0], (BATCH, SEQ, D_MODEL), 1.0),
        'x_sample': nrm(ks[1], (DEC_BATCH, DEC_SEQ, D_MODEL), 1.0),
        'state_gdn_rec': nrm(ks[2], (N_GDN, DEC_BATCH, GDN_V_HEADS, GDN_DK, GDN_DV), 0.1),
        'state_gdn_conv': nrm(ks[3], (N_GDN, DEC_BATCH, CONV_W - 1, GDN_QKV), 1.0),
        'cache_att_k': nrm(ks[4], (N_ATT, DEC_BATCH, rows, ATT_HEADS, ATT_DH), 1.0),
        'cache_att_v': nrm(ks[5], (N_ATT, DEC_BATCH, rows, ATT_HEADS, ATT_DH), 1.0),
        'norm_mix': 1.0 + nrm(ks[6], (DEPTH, D_MODEL), 0.02),
        'norm_ffn': 1.0 + nrm(ks[7], (DEPTH, D_MODEL), 0.02),
        'norm_final': 1.0 + nrm(ks[8], (D_MODEL,), 0.02),
        'gdn_w_in': nrm(ks[9], (N_GDN, D_MODEL, GDN_PROJ), D_MODEL ** -0.5),
        'gdn_w_conv': nrm(ks[11], (N_GDN, CONV_W, GDN_QKV), 0.5),
        'gdn_a_log': jnp.log(jax.random.uniform(ks[12], (N_GDN, GDN_V_HEADS), f32, 1.0, 16.0)),
        'gdn_dt_bias': dt + jnp.log(-jnp.expm1(-dt)),
        'gdn_w_norm': 1.0 + nrm(ks[13], (N_GDN, GDN_DV), 0.02),
        'gdn_w_out': nrm(ks[14], (N_GDN, GDN_Z, D_MODEL), GDN_Z ** -0.5),
        'att_w_qkv': nrm(ks[15], (N_ATT, D_MODEL, 3 * D_MODEL), D_MODEL ** -0.5),
        'att_b_qkv': nrm(ks[16], (N_ATT, 3 * D_MODEL), 0.02),
        'att_rel_bias': nrm(ks[17], (N_ATT, ATT_HEADS, 2 * MAX_REL + 1), 0.5),
        'att_w_o': nrm(ks[18], (N_ATT, D_MODEL, D_MODEL), D_MODEL ** -0.5),
        'att_b_o': nrm(ks[19], (N_ATT, D_MODEL), 0.02),
        'ffn_w_gate': nrm(ks[20], (DEPTH, D_MODEL, D_FF), D_MODEL ** -0.5),
        'ffn_w_up': nrm(ks[21], (DEPTH, D_MODEL, D_FF), D_MODEL ** -0.5),
        'ffn_w_down': nrm(ks[22], (DEPTH, D_FF, D_MODEL), D_FF ** -0.5),
    }


def reference(x_prompt, x_sample, state_gdn_rec, state_gdn_conv, cache_att_k, cache_att_v,
              norm_mix, norm_ffn, norm_final,
              gdn_w_in, gdn_w_conv, gdn_a_log, gdn_dt_bias, gdn_w_norm, gdn_w_out,
              att_w_qkv, att_b_qkv, att_rel_bias, att_w_o, att_b_o,
              ffn_w_gate, ffn_w_up, ffn_w_down):
    xp, xs = x_prompt, x_sample
    bp = xp.shape[0]
    p_rec, p_conv, p_k, p_v = [], [], [], []
    s_rec, s_conv, s_k, s_v = [], [], [], []
    for layer in range(DEPTH):
        j = layer // N_MIXERS
        hp = rmsnorm(xp, norm_mix[layer])
        hs = rmsnorm(xs, norm_mix[layer])
        if layer % N_MIXERS == 0:
            w = (gdn_w_in[j], gdn_w_conv[j], gdn_a_log[j], gdn_dt_bias[j], gdn_w_norm[j], gdn_w_out[j])
            conv0 = jnp.zeros((bp, CONV_W - 1, GDN_QKV), hp.dtype)
            rec0 = jnp.zeros((bp, GDN_V_HEADS, GDN_DK, GDN_DV), jnp.float32)
            yp, cp, rp = gdn_mixer(hp, conv0, rec0, CHUNK, *w)
            ys, cs, rs = gdn_mixer(hs, state_gdn_conv[j], state_gdn_rec[j], xs.shape[1], *w)
            p_conv.append(cp)
            p_rec.append(rp.astype(x_prompt.dtype))
            s_conv.append(cs.astype(state_gdn_conv.dtype))
            s_rec.append(rs.astype(state_gdn_rec.dtype))
        else:
            w = (att_w_qkv[j], att_b_qkv[j], att_rel_bias[j], att_w_o[j], att_b_o[j])
            yp, kp, vp = att_prompt(hp, *w)
            ys, kn, vn = att_sample(hs, cache_att_k[j], cache_att_v[j], *w)
            p_k.append(kp)
            p_v.append(vp)
            s_k.append(kn)
            s_v.append(vn)
        xp = xp + yp
        xs = xs + ys
        xp = xp + swiglu(rmsnorm(xp, norm_ffn[layer]), ffn_w_gate[layer], ffn_w_up[layer], ffn_w_down[layer])
        xs = xs + swiglu(rmsnorm(xs, norm_ffn[layer]), ffn_w_gate[layer], ffn_w_up[layer], ffn_w_down[layer])
    y_prompt = rmsnorm(xp, norm_final)
    y_sample = rmsnorm(xs, norm_final)
    prompt_gdn_rec = jnp.stack(p_rec, axis=0)
    prompt_gdn_conv = jnp.stack(p_conv, axis=0)
    prompt_att_k = jnp.stack(p_k, axis=0)
    prompt_att_v = jnp.stack(p_v, axis=0)
    sample_gdn_rec = jnp.stack(s_rec, axis=0)
    sample_gdn_conv = jnp.stack(s_conv, axis=0)
    sample_att_k = jnp.stack(s_k, axis=0)
    sample_att_v = jnp.stack(s_v, axis=0)
    return (y_prompt, y_sample, prompt_gdn_rec, prompt_gdn_conv, prompt_att_k, prompt_att_v,
            sample_gdn_rec, sample_gdn_conv, sample_att_k, sample_att_v)
```

```python
import numpy as np
from contextlib import ExitStack
import ml_dtypes
import concourse.bass as bass
import concourse.mybir as mybir
from concourse.bass_utils import run_bass_kernel_spmd

F32 = mybir.dt.float32
BF16 = mybir.dt.bfloat16
AF = mybir.ActivationFunctionType
ALU = mybir.AluOpType
AX = mybir.AxisListType

NCORES = 8
D = 2048
KD = D // 128
DEPTH = 4
DEC_B = 16
DEC_S = 32
EPS = 1e-6
GH = 4
GQ = 2
GCOLS = 1544
AH = 2
DFF = 5632
FFC = DFF // NCORES
TT = 256
BAND = 576
NEG = -30000.0

ENGS = ("pe", "act", "dve", "pool", "sp")
DMA_RING = 8


class Buf:
    def __init__(self, name="", excl=False):
        self.name = name
        self.excl = excl
        self.whole_w = None
        self.whole_r = []
        self.parts = {}


class _Rec:
    def __init__(self):
        self.call = None

    def __getattr__(self, name):
        def f(*a, **kw):
            self.call = (name, a, kw)
            return self
        return f


class Prog:
    def __init__(self, nc):
        self.nc = nc
        self.stream = {e: [] for e in ENGS}
        self.cnt = {e: 0 for e in ENGS}
        self.sem = {}
        self.dsem = {}
        self.dcnt = {}
        self.duse = {}
        self.seen = {e: {} for e in ENGS}
        self.alltoks = []

    def open(self, stack):
        for e in ENGS:
            self.sem[e] = stack.enter_context(self.nc.semaphore(f"c_{e}"))
        for e in ("sp", "act", "pool", "cc"):
            self.dsem[e] = [stack.enter_context(self.nc.semaphore(f"d_{e}{i}")) for i in range(DMA_RING)]
            self.dcnt[e] = 0
            for i in range(DMA_RING):
                self.duse[(e, i)] = 0

    def _wait(self, e, tok):
        if tok is None:
            return
        kind, te, slot, val = tok
        key = (kind, te, slot)
        if self.seen[e].get(key, 0) >= val:
            return
        self.seen[e][key] = val
        sem = self.sem[te] if kind == "c" else self.dsem[te][slot]
        self.stream[e].append(("wait", sem, val))

    @staticmethod
    def _bk(x):
        return x if isinstance(x, tuple) else (x, None)

    def _deps(self, reads, writes):
        toks = []
        for r in reads:
            b, k = self._bk(r)
            toks.append(b.whole_w)
            if k is None:
                for p in b.parts.values():
                    toks.append(p[0])
            elif k in b.parts:
                toks.append(b.parts[k][0])
        for w in writes:
            b, k = self._bk(w)
            toks.append(b.whole_w)
            toks.extend(b.whole_r)
            if k is None:
                for p in b.parts.values():
                    toks.append(p[0])
                    toks.extend(p[1])
            elif k in b.parts:
                toks.append(b.parts[k][0])
                toks.extend(b.parts[k][1])
        return toks

    def _commit(self, tok, reads, writes):
        for r in reads:
            b, k = self._bk(r)
            if k is None:
                b.whole_r.append(tok)
            else:
                b.parts.setdefault(k, [None, []])[1].append(tok)
        for w in writes:
            b, k = self._bk(w)
            if k is None:
                b.whole_w = tok
                b.whole_r = []
                b.parts = {}
            else:
                b.parts[k] = [tok, []]

    serial = False

    def op(self, e, fn, reads=(), writes=(), dma=False, ring=None):
        if self.serial:
            for e2 in ENGS:
                if self.cnt[e2] > 0:
                    self._wait(e, ("c", e2, 0, self.cnt[e2]))
            for (r_, slot_), uses_ in self.duse.items():
                if uses_ > 0:
                    self._wait(e, ("d", r_, slot_, (1 if r_ == "cc" else 16) * uses_))
        rec = _Rec()
        fn(rec)
        fn = rec.call
        ex = [r for r in reads if self._bk(r)[0].excl]
        if ex:
            reads = [r for r in reads if not self._bk(r)[0].excl]
            writes = list(writes) + ex
        toks = self._deps(reads, writes)
        if dma:
            r = ring or e
            inc = 1 if r == "cc" else 16
            slot = self.dcnt[r] % DMA_RING
            self.dcnt[r] += 1
            uses = self.duse[(r, slot)]
            if uses > 0:
                self._wait(e, ("d", r, slot, inc * uses))
            self.duse[(r, slot)] = uses + 1
            tok = ("d", r, slot, inc * (uses + 1))
            for t in toks:
                self._wait(e, t)
            self.stream[e].append(("op", fn, self.dsem[r][slot], inc))
        else:
            for t in toks:
                if t is not None and e == "pe" and t[0] == "c" and t[1] == "pe":
                    continue
                self._wait(e, t)
            self.cnt[e] += 1
            tok = ("c", e, 0, self.cnt[e])
            self.stream[e].append(("op", fn, self.sem[e], 1))
        self._commit(tok, reads, writes)
        return tok

    def barrier(self):
        toks = [("c", e, 0, self.cnt[e]) for e in ENGS if self.cnt[e] > 0]
        for (r, slot), uses in self.duse.items():
            if uses > 0:
                toks.append(("d", r, slot, (1 if r == "cc" else 16) * uses))
        for e in ENGS:
            for t in toks:
                if t[0] == "c" and t[1] == e:
                    continue
                self._wait(e, t)

    def emit(self):
        nc = self.nc
        with nc.Block() as block:
            def mk(e):
                def body(engine):
                    for it in self.stream[e]:
                        if it[0] == "wait":
                            engine.wait_ge(it[1], it[2])
                        else:
                            nm, a, kw = it[1]
                            getattr(engine, nm)(*a, **kw).then_inc(it[2], it[3])
                return body
            block.tensor(mk("pe"))
            block.scalar(mk("act"))
            block.vector(mk("dve"))
            block.gpsimd(mk("pool"))
            block.sync(mk("sp"))


SERIAL = False


def build(SEQ, depth=DEPTH, dbg=False):
    NT = SEQ + DEC_B * DEC_S
    NTILE = NT // TT
    NPT = SEQ // TT
    assert SEQ % TT == 0 and SEQ >= 512
    nc = bass.Bass("TRN2", target_bir_lowering=False)
    st = ExitStack()
    P = Prog(nc)
    P.serial = SERIAL
    P.open(st)

    def din(name, shape, dt=F32):
        return nc.dram_tensor(name, list(shape), dt, kind="ExternalInput").ap()

    def dout(name, shape, dt=F32):
        return nc.dram_tensor(name, list(shape), dt, kind="ExternalOutput").ap()

    def dint(name, shape, dt):
        return nc.dram_tensor(name, list(shape), dt, kind="Internal").ap()

    xT_in = din("xT_in", [D, NT])
    cst_f = din("cst_f", [128, 3, 128])
    ones_f_d = din("ones_f", [128, 128])
    gam_mix = din("gam_mix", [128, DEPTH, KD])
    gam_ffn = din("gam_ffn", [128, DEPTH, KD])
    gam_fin = din("gam_fin", [128, KD])
    g_win = din("g_win", [2, D, GCOLS])
    g_wconv = din("g_wconv", [2, 128, 8, 4])
    g_alog = din("g_alog", [2, 128, GH])
    g_dtb = din("g_dtb", [2, 128, GH])
    g_wnorm = din("g_wnorm", [2, 128, 1])
    g_wout = din("g_wout", [2, 4096, 256])
    g_s0 = din("g_s0", [2, DEC_B, 128, GH, 128])
    g_c0 = din("g_c0", [2, 128, 8, DEC_B, 3])
    a_wqkv = din("a_wqkv", [2, D, 768])
    a_bqk = din("a_bqk", [2, 128, 4])
    a_bv = din("a_bv", [2, 64, 256])
    a_bias = din("a_bias", [2, 64, AH, BAND])
    a_wo = din("a_wo", [2, D, 256])
    a_bo = din("a_bo", [128, 2, KD])
    a_ck = din("a_ck", [2, DEC_B, 128, AH, 512])
    a_cv = din("a_cv", [2, DEC_B, 64, 8, AH, 128])
    f_wg = din("f_wg", [DEPTH, D, FFC])
    f_wu = din("f_wu", [DEPTH, D, FFC])
    f_wd = din("f_wd", [DEPTH, DFF, 256])
    yT = dout("yT", [D, NT])
    o_prec = dout("o_prec", [2, 128, GH, 128])
    o_pconv = dout("o_pconv", [2, 128, 8, 3])
    o_srec = dout("o_srec", [2, DEC_B, 128, GH, 128])
    o_sconv = dout("o_sconv", [2, 128, 8, DEC_B, 3])
    o_pk = dout("o_pk", [2, 128, AH, 512])
    o_pv = dout("o_pv", [2, 64, 8, AH, 128])
    o_sk = dout("o_sk", [2, 128, AH, DEC_B * DEC_S])
    o_sv = dout("o_sv", [2, DEC_B, DEC_S, AH, 128])
    xT = dint("xT_res", [D, NT], F32)
    o_loc = dint("o_loc", [512, NT], BF16)
    o_all = dint("o_all", [4096, NT], BF16)
    oa_loc = dint("oa_loc", [256, NT], BF16)
    oa_all = dint("oa_all", [2048, NT], BF16)
    y_loc = dint("y_loc", [256, NT], F32)
    y_all = dint("y_all", [D, NT], F32)
    act_loc = dint("act_loc", [FFC, NT], BF16)
    act_all = dint("act_all", [DFF, NT], BF16)
    B_xT = Buf("xT"); B_oloc = Buf(); B_oall = Buf(); B_oaloc = Buf(); B_oaall = Buf()
    B_yloc = Buf(); B_yall = Buf(); B_aloc = Buf(); B_aall = Buf()
    B_out = Buf("outputs")
    GROUPS = [list(range(NCORES))]

    uid = [0]

    def sb(name, shape, dt, stack):
        uid[0] += 1
        return stack.enter_context(nc.sbuf_tensor(f"s{uid[0]}_{name}", list(shape), dt))

    banks = [st.enter_context(nc.psum_tensor(f"ps{i}", [128, 512], F32)) for i in range(8)]
    bbufs = [Buf(f"ps{i}", excl=True) for i in range(8)]
    bstate = [0]

    def bank():
        i = bstate[0] % 8
        bstate[0] += 1
        return banks[i], bbufs[i]

    cf = sb("cf", [128, 3, 128], F32, st); B_c = Buf("const")
    onesf = sb("onesf", [128, 128], F32, st)
    onesb = sb("onesb", [128, 128], BF16, st)
    identb = sb("identb", [128, 128], BF16, st)
    gm = sb("gm", [128, DEPTH, KD], F32, st)
    gf = sb("gf", [128, DEPTH, KD], F32, st)
    gfin = sb("gfin", [128, KD], F32, st)
    bo_t = sb("bo_t", [128, 2, KD], F32, st)
    P.op("sp", lambda e: e.dma_start(out=cf[:], in_=cst_f), writes=[(B_c, 0)], dma=True)
    P.op("sp", lambda e: e.dma_start(out=onesf[:], in_=ones_f_d), writes=[(B_c, 1)], dma=True)
    P.op("sp", lambda e: e.dma_start(out=gm[:], in_=gam_mix), writes=[(B_c, 2)], dma=True)
    P.op("sp", lambda e: e.dma_start(out=gf[:], in_=gam_ffn), writes=[(B_c, 3)], dma=True)
    P.op("sp", lambda e: e.dma_start(out=gfin[:], in_=gam_fin), writes=[(B_c, 4)], dma=True)
    P.op("sp", lambda e: e.dma_start(out=bo_t[:], in_=a_bo), writes=[(B_c, 5)], dma=True)
    P.op("dve", lambda e: e.tensor_copy(out=onesb[:], in_=onesf[:]), reads=[(B_c, 1)], writes=[(B_c, 6)])
    P.op("dve", lambda e: e.tensor_copy(out=identb[:], in_=cf[:, 0, :]), reads=[(B_c, 0)], writes=[(B_c, 7)])
    ident_f = cf[:, 0, :]
    mask_i = cf[:, 1, :]
    mask_s = cf[:, 2, :]
    P.barrier()
    RC = [B_c]

    xv = lambda ap: ap.rearrange("(k p) n -> p k n", p=128)
    dstate = {"on": False}

    def ddump(name, ap, B, dt=F32):
        if not (dbg and dstate["on"]):
            return
        d = dout("dd_" + name, list(ap.shape), dt)
        P.op("sp", lambda e: e.dma_start(out=d, in_=ap), reads=[B], writes=[B_out], dma=True)

    class NormCtx:
        def __init__(self, stack):
            self.x = sb("n_x", [128, KD, TT], F32, stack); self.Bx = Buf()
            self.y = sb("n_y", [128, KD, TT], F32, stack); self.By = Buf()
            self.sq = sb("n_sq", [128, KD, TT], BF16, stack); self.Bsq = Buf()
            self.rs = sb("n_rs", [128, TT], F32, stack); self.Brs = Buf()
            self.h = [sb(f"n_h{i}", [128, KD, TT], BF16, stack) for i in range(2)]
            self.Bh = [Buf(), Buf()]
            self.i = 0

        def run(self, t, src, gamma, add=None, bias=None, h_f32=None):
            c0 = t * TT
            x, y, sq, rs = self.x, self.y, self.sq, self.rs
            P.op("sp", lambda e: e.dma_start(out=x[:], in_=xv(src)[:, :, c0:c0 + TT]),
                 reads=[B_xT], writes=[self.Bx], dma=True)
            if add is not None:
                P.op("act", lambda e: e.dma_start(out=y[:], in_=xv(add)[:, :, c0:c0 + TT]),
                     reads=[B_yall], writes=[self.By], dma=True)
                if bias is None:
                    P.op("pool", lambda e: e.tensor_tensor(out=x[:], in0=x[:], in1=y[:], op=ALU.add),
                         reads=[self.By], writes=[self.Bx])
                else:
                    for k in range(KD):
                        P.op("dve", lambda e, k=k: e.scalar_tensor_tensor(
                            out=x[:, k, :], in0=y[:, k, :], scalar=bias[:, k:k + 1], in1=x[:, k, :],
                            op0=ALU.add, op1=ALU.add), reads=[self.By] + RC, writes=[self.Bx])
                P.op("sp", lambda e: e.dma_start(out=xv(xT)[:, :, c0:c0 + TT], in_=x[:]),
                     reads=[self.Bx], writes=[B_xT], dma=True)
            P.op("act", lambda e: e.activation(out=sq[:], in_=x[:], func=AF.Square),
                 reads=[self.Bx], writes=[self.Bsq])
            ps, bp = bank()
            for k in range(KD):
                P.op("pe", lambda e, k=k: e.matmul(ps[:, 0:TT], lhsT=onesb[:], rhs=sq[:, k, :],
                                                   start=(k == 0), stop=(k == KD - 1)),
                     reads=[self.Bsq] + RC, writes=[bp])
            P.op("act", lambda e: e.activation(out=rs[:], in_=ps[:, 0:TT], func=AF.Sqrt,
                                               scale=1.0 / D, bias=EPS), reads=[bp], writes=[self.Brs])
            P.op("dve", lambda e: e.reciprocal(out=rs[:], in_=rs[:]), reads=[self.Brs], writes=[self.Brs])
            if h_f32 is not None:
                h, bh = h_f32
            else:
                h, bh = self.h[self.i % 2], self.Bh[self.i % 2]
                self.i += 1
            for k in range(KD):
                eng = "dve"
                P.op(eng, lambda e, k=k: e.scalar_tensor_tensor(
                    out=h[:, k, :], in0=x[:, k, :], scalar=gamma[:, k:k + 1], in1=rs[:],
                    op0=ALU.mult, op1=ALU.mult), reads=[self.Bx, self.Brs] + RC, writes=[(bh, k)])
            return h, bh

    def load_w_bf16(dst, dram_ap, B):
        K = dst.shape[1]
        for k0 in range(0, K, 4):
            k1 = min(K, k0 + 4)
            P.op("pool", lambda e, k0=k0, k1=k1: e.dma_start(
                out=dst[:, k0:k1, :], in_=xv(dram_ap)[:, k0:k1, :]), writes=[(B, k0)], dma=True)

    def allgather(src, Bs, dst, Bd):
        P.op("pool", lambda e: e.collective_compute("AllGather", ALU.bypass, replica_groups=GROUPS,
                                                   ins=[src.opt()], outs=[dst.opt()]),
             reads=[Bs], writes=[Bd], dma=True, ring="cc")

    def dense_out(o_src, Bsrc, KT, w_dram, bias_cols=None):
        with ExitStack() as s2:
            w = sb("do_w", [128, KT, 256], BF16, s2); Bw = Buf()
            load_w_bf16(w, w_dram, Bw)
            ins = [sb(f"do_in{i}", [128, KT, TT], BF16, s2) for i in range(2)]; Bin = [Buf(), Buf()]
            yo = [sb(f"do_y{i}", [128, 2, TT], F32, s2) for i in range(2)]; Byo = [Buf(), Buf()]
            for t in range(NTILE):
                c0 = t * TT
                it, Bi = ins[t % 2], Bin[t % 2]
                yt, By = yo[t % 2], Byo[t % 2]
                P.op("sp", lambda e, it=it, c0=c0: e.dma_start(out=it[:], in_=xv(o_src)[:, :, c0:c0 + TT]),
                     reads=[Bsrc], writes=[Bi], dma=True)
                for ct in range(2):
                    ps, bp = bank()
                    for k in range(KT):
                        P.op("pe", lambda e, k=k, ct=ct, it=it, ps=ps: e.matmul(
                            ps[:, 0:TT], lhsT=w[:, k, ct * 128:(ct + 1) * 128], rhs=it[:, k, :],
                            start=(k == 0), stop=(k == KT - 1)), reads=[Bw, Bi], writes=[bp])
                    P.op("act", lambda e, ct=ct, yt=yt, ps=ps: e.activation(
                        out=yt[:, ct, :], in_=ps[:, 0:TT], func=AF.Copy), reads=[bp], writes=[(By, ct)])
                P.op("sp", lambda e, yt=yt, c0=c0: e.dma_start(
                    out=y_loc.rearrange("(c p) n -> p c n", p=128)[:, :, c0:c0 + TT], in_=yt[:]),
                    reads=[By], writes=[B_yloc], dma=True)
            P.barrier()
        allgather(y_loc, B_yloc, y_all, B_yall)

    def ffn_layer(layer, src, add_bias):
        with ExitStack() as s2:
            nctx = NormCtx(s2)
            wg = sb("f_wg", [128, KD, FFC], BF16, s2); Bwg = Buf()
            wu = sb("f_wu", [128, KD, FFC], BF16, s2); Bwu = Buf()
            load_w_bf16(wg, f_wg[layer], Bwg)
            load_w_bf16(wu, f_wu[layer], Bwu)
            sg = [sb(f"f_sg{i}", [128, TT], F32, s2) for i in range(2)]; Bsg = [Buf(), Buf()]
            at = [sb(f"f_at{i}", [128, 6, TT], BF16, s2) for i in range(2)]; Bat = [Buf(), Buf()]
            ftiles = [(i * 128, 128) for i in range(5)] + [(640, 64)]
            for t in range(NTILE):
                c0 = t * TT
                h, bh = nctx.run(t, src, gf[:, layer, :], add=y_all, bias=add_bias)
                a, Ba = at[t % 2], Bat[t % 2]
                for fi, (f0, fw) in enumerate(ftiles):
                    pg, bg = bank()
                    pu, bu = bank()
                    for k in range(KD):
                        P.op("pe", lambda e, k=k, pg=pg, f0=f0, fw=fw: e.matmul(
                            pg[0:fw, 0:TT], lhsT=wg[:, k, f0:f0 + fw], rhs=h[:, k, :],
                            start=(k == 0), stop=(k == KD - 1)), reads=[Bwg, bh], writes=[bg])
                    for k in range(KD):
                        P.op("pe", lambda e, k=k, pu=pu, f0=f0, fw=fw: e.matmul(
                            pu[0:fw, 0:TT], lhsT=wu[:, k, f0:f0 + fw], rhs=h[:, k, :],
                            start=(k == 0), stop=(k == KD - 1)), reads=[Bwu, bh], writes=[bu])
                    s, Bs = sg[fi % 2], Bsg[fi % 2]
                    P.op("act", lambda e, s=s, pg=pg, fw=fw: e.activation(
                        out=s[0:fw, :], in_=pg[0:fw, 0:TT], func=AF.Silu), reads=[bg], writes=[Bs])
                    P.op("dve", lambda e, s=s, pu=pu, fw=fw, a=a, fi=fi: e.tensor_tensor(
                        out=a[0:fw, fi, :], in0=pu[0:fw, 0:TT], in1=s[0:fw, :], op=ALU.mult),
                        reads=[bu, Bs], writes=[(Ba, fi)])
                P.op("sp", lambda e, a=a, c0=c0: e.dma_start(
                    out=act_loc[0:640, :].rearrange("(c p) n -> p c n", p=128)[:, :, c0:c0 + TT],
                    in_=a[:, 0:5, :]), reads=[Ba], writes=[B_aloc], dma=True)
                P.op("sp", lambda e, a=a, c0=c0: e.dma_start(
                    out=act_loc[640:704, c0:c0 + TT], in_=a[0:64, 5, :]), reads=[Ba], writes=[B_aloc], dma=True)
            P.barrier()
        allgather(act_loc, B_aloc, act_all, B_aall)
        dense_out(act_all, B_aall, DFF // 128, f_wd[layer])

    def gdn_layer(j, layer, src, add, add_bias):
        with ExitStack() as s2:
            nctx = NormCtx(s2)
            w = sb("g_w", [128, KD, GCOLS], BF16, s2); Bw = Buf()
            load_w_bf16(w, g_win[j], Bw)
            wc = sb("g_wc", [128, 8, 4], F32, s2)
            alog = sb("g_alog", [128, GH], F32, s2)
            dtb = sb("g_dtb", [128, GH], F32, s2)
            wnm = sb("g_wnm", [128, 1], F32, s2)
            Bk = Buf()
            P.op("sp", lambda e: e.dma_start(out=wc[:], in_=g_wconv[j]), writes=[(Bk, 0)], dma=True)
            P.op("sp", lambda e: e.dma_start(out=alog[:], in_=g_alog[j]), writes=[(Bk, 1)], dma=True)
            P.op("sp", lambda e: e.dma_start(out=dtb[:], in_=g_dtb[j]), writes=[(Bk, 2)], dma=True)
            P.op("sp", lambda e: e.dma_start(out=wnm[:], in_=g_wnorm[j]), writes=[(Bk, 3)], dma=True)
            negA = sb("g_negA", [128, GH], F32, s2)
            P.op("act", lambda e: e.activation(out=negA[:], in_=alog[:], func=AF.Exp), reads=[(Bk, 1)], writes=[(Bk, 4)])
            P.op("dve", lambda e: e.tensor_scalar(out=negA[:], in0=negA[:], scalar1=-1.0, scalar2=None,
                                                  op0=ALU.mult), reads=[(Bk, 4)], writes=[(Bk, 4)])
            RK = [Bk]
            xc = sb("g_xc", [128, 8, TT + 24], F32, s2); Bxc = Buf()
            acc = sb("g_acc", [128, TT], F32, s2); Bacc = Buf()
            cs = sb("g_cs", [128, TT], F32, s2); Bcs = Buf()
            csq = sb("g_csq", [128, TT], BF16, s2); Bcsq = Buf()
            rn = sb("g_rn", [128, TT], F32, s2); Brn = Buf()
            qT = sb("g_qT", [128, GQ, TT], BF16, s2); BqT = Buf()
            kT = sb("g_kT", [128, GQ, TT], BF16, s2); BkT = Buf()
            vT = sb("g_vT", [128, GH, TT], BF16, s2); BvT = Buf()
            szT = sb("g_szT", [128, GH, TT], BF16, s2); BszT = Buf()
            ogT = [sb(f"g_ogT{i}", [128, GH, TT], BF16, s2) for i in range(2)]; BogT = [Buf(), Buf()]
            ba = sb("g_ba", [128, 8], F32, s2); Bba = Buf()
            gt = sb("g_g", [128, GH], F32, s2); Bg = Buf()
            bet = sb("g_bet", [128, GH], F32, s2); Bbet = Buf()
            nbet = sb("g_nbet", [128, GH], F32, s2)
            S = sb("g_S", [128, GH, 128], F32, s2); BS = Buf()
            Sb = sb("g_Sb", [128, GH, 128], BF16, s2); BSb = Buf()
            def T4(name, dt=F32):
                return sb(name, [128, GH, 128], dt, s2), Buf()
            gc, Bgc = sb("c_gc", [128, GH], F32, s2), Buf()
            gl, Bgl = sb("c_gl", [128, GH], F32, s2), Buf()
            edl, Bedl = sb("c_edl", [128, GH], F32, s2), Buf()
            egl, Begl = sb("c_egl", [128, GH], F32, s2), Buf()
            gb, Bgb = T4("c_gb")
            Dm, BD = T4("c_D")
            DecI, BDecI = T4("c_DecI")
            DecS, BDecS = T4("c_DecS")
            egbc, Begbc = T4("c_egbc")
            KgT, BKgT = T4("c_KgT", BF16)
            QgT, BQgT = T4("c_QgT", BF16)
            QKT, BQKT = T4("c_QKT", BF16)
            Pm = [T4("c_P0"), T4("c_P1")]
            PTm = [T4("c_PT0"), T4("c_PT1")]
            Ym = [T4("c_Y0"), T4("c_Y1")]
            XT, BXT = T4("c_XT", BF16)
            vtm, Bvtm = T4("c_vtm", BF16)
            ktm, Bktm = sb("c_ktm", [128, GQ, 128], BF16, s2), Buf()
            kdec, Bkdec = T4("c_kdec", BF16)
            rr, Brr = T4("c_r", BF16)
            vn, Bvn = T4("c_vn", BF16)
            osc, Bosc = T4("c_osc", BF16)
            junk, Bjunk = sb("c_junk", [128, 128], F32, s2), Buf()
            ss, Bss = sb("c_ss", [128, GH], F32, s2), Buf()

            def chunk(C, col0, og, Bog, ocol0):
                RQ = [BqT, BkT]
                cs_ = slice(col0, col0 + C)
                ps, bp = bank()
                for a in range(GQ):
                    P.op("pe", lambda e, a=a, ps=ps: e.matmul(ps[0:C, a * 128:(a + 1) * 128], lhsT=kT[:, a, cs_],
                                                              rhs=identb[:], start=True, stop=True),
                         reads=RQ + RC, writes=[bp])
                P.op("act", lambda e, ps=ps: e.activation(out=ktm[0:C].rearrange("p a d -> p (a d)"),
                                                          in_=ps[0:C, 0:256], func=AF.Copy), reads=[bp], writes=[Bktm])
                ps, bp = bank()
                for h in range(GH):
                    P.op("pe", lambda e, h=h, ps=ps: e.matmul(ps[0:C, h * 128:(h + 1) * 128], lhsT=vT[:, h, cs_],
                                                              rhs=identb[:], start=True, stop=True),
                         reads=[BvT] + RC, writes=[bp])
                P.op("act", lambda e, ps=ps: e.activation(out=vtm[0:C].rearrange("p a d -> p (a d)"),
                                                          in_=ps[0:C, :], func=AF.Copy), reads=[bp], writes=[Bvtm])
                ps, bp = bank()
                P.op("pe", lambda e, ps=ps: e.matmul(ps[0:C, 0:GH], lhsT=mask_i[0:C, 0:C], rhs=gt[0:C, :],
                                                     start=True, stop=True), reads=[Bg] + RC, writes=[bp])
                P.op("pe", lambda e, ps=ps: e.matmul(ps[:, 8:8 + GH], lhsT=onesf[0:C, :], rhs=gt[0:C, :],
                                                     start=True, stop=True), reads=[Bg] + RC, writes=[bp])
                P.op("dve", lambda e, ps=ps: e.tensor_copy(out=gc[0:C, :], in_=ps[0:C, 0:GH]), reads=[bp], writes=[Bgc])
                P.op("dve", lambda e, ps=ps: e.tensor_copy(out=gl[:], in_=ps[:, 8:8 + GH]), reads=[bp], writes=[Bgl])
                P.op("dve", lambda e: e.tensor_tensor(out=edl[0:C, :], in0=gl[0:C, :], in1=gc[0:C, :], op=ALU.subtract),
                     reads=[Bgl, Bgc], writes=[Bedl])
                P.op("act", lambda e: e.activation(out=edl[0:C, :], in_=edl[0:C, :], func=AF.Exp), reads=[Bedl], writes=[Bedl])
                P.op("act", lambda e: e.activation(out=egl[:], in_=gl[:], func=AF.Exp), reads=[Bgl], writes=[Begl])
                for h in range(GH):
                    P.op("pool", lambda e, h=h: e.tensor_scalar(out=gb[0:C, h, :], in0=onesf[0:C, :],
                                                               scalar1=gt[0:C, h:h + 1], scalar2=None, op0=ALU.mult),
                         reads=[Bg] + RC, writes=[(Bgb, h)])
                pB, bB = bank()
                for h in range(GH):
                    P.op("pe", lambda e, h=h, pB=pB: e.matmul(pB[:, h * 128:h * 128 + C], lhsT=gb[0:C, h, :],
                                                              rhs=mask_i[0:C, 0:C], start=True, stop=True),
                         reads=[Bgb] + RC, writes=[bB])
                for h in range(GH):
                    P.op("dve", lambda e, h=h, pB=pB: e.tensor_scalar(
                        out=Dm[0:C, h, 0:C], in0=pB[0:C, h * 128:h * 128 + C], scalar1=gc[0:C, h:h + 1], scalar2=0.0,
                        op0=ALU.subtract, op1=ALU.min), reads=[bB, Bgc], writes=[(BD, h)])
                P.op("act", lambda e: e.activation(out=Dm[0:C, :, 0:C], in_=Dm[0:C, :, 0:C], func=AF.Exp),
                     reads=[BD], writes=[BD])
                for h in range(GH):
                    P.op("pool", lambda e, h=h: e.tensor_tensor(out=DecI[0:C, h, 0:C], in0=Dm[0:C, h, 0:C],
                                                               in1=mask_i[0:C, 0:C], op=ALU.mult),
                         reads=[BD] + RC, writes=[(BDecI, h)])
                    P.op("pool", lambda e, h=h: e.tensor_tensor(out=DecS[0:C, h, 0:C], in0=Dm[0:C, h, 0:C],
                                                               in1=mask_s[0:C, 0:C], op=ALU.mult),
                         reads=[BD] + RC, writes=[(BDecS, h)])
                for h in range(GH):
                    P.op("act", lambda e, h=h, pB=pB: e.activation(out=egbc[:, h, 0:C], in_=pB[:, h * 128:h * 128 + C],
                                                                   func=AF.Exp), reads=[bB], writes=[(Begbc, h)])
                for h in range(GH):
                    P.op("dve", lambda e, h=h: e.tensor_tensor(out=KgT[:, h, 0:C], in0=kT[:, h // 2, cs_],
                                                              in1=egbc[:, h, 0:C], op=ALU.mult),
                         reads=RQ + [(Begbc, h)], writes=[(BKgT, h)])
                    P.op("pool", lambda e, h=h: e.tensor_tensor(out=QgT[:, h, 0:C], in0=qT[:, h // 2, cs_],
                                                               in1=egbc[:, h, 0:C], op=ALU.mult),
                         reads=RQ + [(Begbc, h)], writes=[(BQgT, h)])
                pC, bC = bank()
                for a in range(GQ):
                    P.op("pe", lambda e, a=a, pC=pC: e.matmul(pC[0:C, a * 128:a * 128 + C], lhsT=kT[:, a, cs_],
                                                              rhs=kT[:, a, cs_], start=True, stop=True),
                         reads=RQ, writes=[bC])
                    P.op("pe", lambda e, a=a, pC=pC: e.matmul(pC[0:C, 256 + a * 128:256 + a * 128 + C],
                                                              lhsT=kT[:, a, cs_], rhs=qT[:, a, cs_], start=True, stop=True),
                         reads=RQ, writes=[bC])
                Pc, BPc = Pm[0]
                for h in range(GH):
                    a = h // 2
                    P.op("dve", lambda e, h=h, a=a, pC=pC, Pc=Pc: e.scalar_tensor_tensor(
                        out=Pc[0:C, h, 0:C], in0=pC[0:C, a * 128:a * 128 + C], scalar=nbet[0:C, h:h + 1],
                        in1=DecS[0:C, h, 0:C], op0=ALU.mult, op1=ALU.mult), reads=[bC, Bbet, (BDecS, h)], writes=[(BPc, h)])
                    P.op("dve", lambda e, h=h, a=a, pC=pC: e.tensor_tensor(
                        out=QKT[0:C, h, 0:C], in0=pC[0:C, 256 + a * 128:256 + a * 128 + C], in1=DecI[0:C, h, 0:C],
                        op=ALU.mult), reads=[bC, (BDecI, h)], writes=[(BQKT, h)])
                pT, bT = bank()
                for h in range(GH):
                    P.op("pe", lambda e, h=h, pT=pT, Pc=Pc: e.matmul(pT[0:C, h * 128:h * 128 + C], lhsT=Pc[0:C, h, 0:C],
                                                                     rhs=ident_f[0:C, 0:C], start=True, stop=True),
                         reads=[BPc] + RC, writes=[bT])
                PTc, BPTc = PTm[0]
                P.op("act", lambda e, pT=pT, PTc=PTc: e.activation(
                    out=PTc[0:C, :, 0:C], in_=pT[0:C, :].rearrange("p (h c) -> p h c", h=GH)[:, :, 0:C], func=AF.Copy),
                    reads=[bT], writes=[BPTc])
                Yc, BYc = Ym[0]
                for h in range(GH):
                    P.op("pool", lambda e, h=h, Yc=Yc, Pc=Pc: e.tensor_tensor(out=Yc[0:C, h, 0:C], in0=Pc[0:C, h, 0:C],
                                                                             in1=ident_f[0:C, 0:C], op=ALU.add),
                         reads=[(BPc, h)] + RC, writes=[(BYc, h)])
                nst = {128: 6, 64: 5, 32: 4}[C]
                cur = 0
                v3 = lambda ps_: ps_[0:C, :].rearrange("p (h c) -> p h c", h=GH)[:, :, 0:C]
                for s_ in range(nst):
                    last = s_ == nst - 1
                    Pc, BPc = Pm[cur]; PTc, BPTc = PTm[cur]; Yc, BYc = Ym[cur]
                    Pn, BPn = Pm[1 - cur]; PTn, BPTn = PTm[1 - cur]; Yn, BYn = Ym[1 - cur]
                    p1, b1 = bank()
                    for h in range(GH):
                        P.op("pe", lambda e, h=h, p1=p1, Pc=Pc, PTc=PTc: e.matmul(
                            p1[0:C, h * 128:h * 128 + C], lhsT=Pc[0:C, h, 0:C], rhs=PTc[0:C, h, 0:C], start=True, stop=True),
                            reads=[BPc, BPTc], writes=[b1])
                    P.op("act", lambda e, p1=p1, PTn=PTn: e.activation(out=PTn[0:C, :, 0:C], in_=v3(p1), func=AF.Copy),
                         reads=[b1], writes=[BPTn])
                    if not last:
                        p2, b2 = bank()
                        for h in range(GH):
                            P.op("pe", lambda e, h=h, p2=p2, Pc=Pc, PTc=PTc: e.matmul(
                                p2[0:C, h * 128:h * 128 + C], lhsT=PTc[0:C, h, 0:C], rhs=Pc[0:C, h, 0:C], start=True, stop=True),
                                reads=[BPc, BPTc], writes=[b2])
                        P.op("dve", lambda e, p2=p2, Pn=Pn: e.tensor_copy(out=Pn[0:C, :, 0:C], in_=v3(p2)),
                             reads=[b2], writes=[BPn])
                    p3, b3 = bank()
                    for h in range(GH):
                        P.op("pe", lambda e, h=h, p3=p3, PTn=PTn, Yc=Yc: e.matmul(
                            p3[0:C, h * 128:h * 128 + C], lhsT=PTn[0:C, h, 0:C], rhs=Yc[0:C, h, 0:C], start=True, stop=True),
                            reads=[BPTn, BYc], writes=[b3])
                    if last:
                        P.op("dve", lambda e, p3=p3, Yc=Yc: e.tensor_tensor(out=XT[0:C, :, 0:C], in0=v3(p3),
                                                                           in1=Yc[0:C, :, 0:C], op=ALU.add),
                             reads=[b3, BYc], writes=[BXT])
                    else:
                        P.op("dve", lambda e, p3=p3, Yc=Yc, Yn=Yn: e.tensor_tensor(out=Yn[0:C, :, 0:C], in0=v3(p3),
                                                                                  in1=Yc[0:C, :, 0:C], op=ALU.add),
                             reads=[b3, BYc], writes=[BYn])
                    cur = 1 - cur
                for h in range(GH):
                    P.op("act", lambda e, h=h: e.activation(out=kdec[0:C, h, :], in_=ktm[0:C, h // 2, :], func=AF.Copy,
                                                            scale=edl[0:C, h:h + 1]), reads=[Bktm, Bedl], writes=[(Bkdec, h)])
                ddump("gt", gt[:], Bg); ddump("bet", bet[:], Bbet); ddump("gc", gc[:], Bgc); ddump("gl", gl[:], Bgl)
                ddump("edl", edl[:], Bedl); ddump("egl", egl[:], Begl); ddump("Dm", Dm[:], BD); ddump("DecS", DecS[:], BDecS)
                ddump("egbc", egbc[:], Begbc); ddump("XT", XT[:], BXT, BF16); ddump("QKT", QKT[:], BQKT, BF16)
                ddump("KgT", KgT[:], BKgT, BF16); ddump("kdec", kdec[:], Bkdec, BF16); ddump("vtm", vtm[:], Bvtm, BF16)
                ddump("P0", Pm[0][0][:], Pm[0][1]); ddump("P1", Pm[1][0][:], Pm[1][1]); ddump("Sb0", Sb[:], BSb, BF16)
                ddump("kT", kT[:], BkT, BF16); ddump("qT", qT[:], BqT, BF16)
                pE, bE = bank()
                for h in range(GH):
                    P.op("pe", lambda e, h=h, pE=pE: e.matmul(pE[0:C, h * 128:(h + 1) * 128], lhsT=KgT[:, h, 0:C],
                                                              rhs=Sb[:, h, :], start=True, stop=True),
                         reads=[BKgT, BSb], writes=[bE])
                P.op("dve", lambda e, pE=pE: e.tensor_tensor(out=rr[0:C].rearrange("p h d -> p (h d)"),
                                                            in0=vtm[0:C].rearrange("p h d -> p (h d)"), in1=pE[0:C, :],
                                                            op=ALU.subtract), reads=[bE, Bvtm], writes=[Brr])
                pF, bF = bank()
                for h in range(GH):
                    P.op("pe", lambda e, h=h, pF=pF: e.matmul(pF[0:C, h * 128:(h + 1) * 128], lhsT=XT[0:C, h, 0:C],
                                                              rhs=rr[0:C, h, :], start=True, stop=True),
                         reads=[BXT, Brr], writes=[bF])
                for h in range(GH):
                    P.op("act", lambda e, h=h, pF=pF: e.activation(out=vn[0:C, h, :], in_=pF[0:C, h * 128:(h + 1) * 128],
                                                                   func=AF.Copy, scale=bet[0:C, h:h + 1]),
                         reads=[bF, Bbet], writes=[(Bvn, h)])
                pG, bG = bank()
                for h in range(GH):
                    P.op("pe", lambda e, h=h, pG=pG: e.matmul(pG[0:C, h * 128:(h + 1) * 128], lhsT=QgT[:, h, 0:C],
                                                              rhs=Sb[:, h, :], start=True, stop=False),
                         reads=[BQgT, BSb], writes=[bG])
                    P.op("pe", lambda e, h=h, pG=pG: e.matmul(pG[0:C, h * 128:(h + 1) * 128], lhsT=QKT[0:C, h, 0:C],
                                                              rhs=vn[0:C, h, :], start=False, stop=True),
                         reads=[BQKT, Bvn], writes=[bG])
                pH, bH = bank()
                for h in range(GH):
                    P.op("pe", lambda e, h=h, pH=pH: e.matmul(pH[:, h * 128:(h + 1) * 128], lhsT=kdec[0:C, h, :],
                                                              rhs=vn[0:C, h, :], start=True, stop=True),
                         reads=[Bkdec, Bvn], writes=[bH])
                for h in range(GH):
                    P.op("dve", lambda e, h=h, pH=pH: e.scalar_tensor_tensor(
                        out=S[:, h, :], in0=S[:, h, :], scalar=egl[:, h:h + 1], in1=pH[:, h * 128:(h + 1) * 128],
                        op0=ALU.mult, op1=ALU.add), reads=[bH, Begl, BSb], writes=[(BS, h)])
                P.op("act", lambda e: e.activation(out=Sb[:], in_=S[:], func=AF.Copy), reads=[BS], writes=[BSb])
                for h in range(GH):
                    P.op("act", lambda e, h=h, pG=pG: e.activation(out=junk[0:C, :], in_=pG[0:C, h * 128:(h + 1) * 128],
                                                                   func=AF.Square, accum_out=ss[0:C, h:h + 1]),
                         reads=[bG], writes=[Bjunk, (Bss, h)])
                P.op("act", lambda e: e.activation(out=ss[0:C, :], in_=ss[0:C, :], func=AF.Sqrt, scale=1.0 / 128, bias=EPS),
                     reads=[Bss], writes=[Bss])
                P.op("dve", lambda e: e.reciprocal(out=ss[0:C, :], in_=ss[0:C, :]), reads=[Bss], writes=[Bss])
                for h in range(GH):
                    P.op("act", lambda e, h=h, pG=pG: e.activation(out=osc[0:C, h, :], in_=pG[0:C, h * 128:(h + 1) * 128],
                                                                   func=AF.Copy, scale=ss[0:C, h:h + 1]),
                         reads=[bG, Bss], writes=[(Bosc, h)])
                pO, bO = bank()
                for h in range(GH):
                    P.op("pe", lambda e, h=h, pO=pO: e.matmul(pO[:, h * 128:h * 128 + C], lhsT=osc[0:C, h, :],
                                                              rhs=identb[0:C, 0:C], start=True, stop=True),
                         reads=[Bosc] + RC, writes=[bO])
                for h in range(GH):
                    P.op("dve", lambda e, h=h, pO=pO: e.scalar_tensor_tensor(
                        out=og[:, h, ocol0:ocol0 + C], in0=pO[:, h * 128:h * 128 + C], scalar=wnm[:, 0:1],
                        in1=szT[:, h, cs_], op0=ALU.mult, op1=ALU.mult), reads=[bO, BszT] + RK, writes=[(Bog, h)])

            for t in range(NTILE):
                c0 = t * TT
                sample = t >= NPT
                h, bh = nctx.run(t, src, gm[:, layer, :], add=add, bias=add_bias)
                og, Bog = ogT[t % 2], BogT[t % 2]
                nseq = TT // DEC_S
                if t == 0:
                    P.op("pool", lambda e: e.memset(xc[:], 0.0), writes=[Bxc])
                    P.op("pool", lambda e: e.memset(S[:], 0.0), writes=[BS])
                    P.op("pool", lambda e: e.memset(Sb[:], 0.0), writes=[BSb])
                if sample:
                    b0 = (t - NPT) * nseq
                    xcv = xc[:, :, 0:nseq * 35].rearrange("p c (b w) -> p c b w", w=35)
                    P.op("sp", lambda e, b0=b0: e.dma_start(out=xcv[:, :, :, 0:3], in_=g_c0[j][:, :, b0:b0 + nseq, :]),
                         writes=[Bxc], dma=True)
                for ct in range(12):
                    ps, bp = bank()
                    for k in range(KD):
                        P.op("pe", lambda e, k=k, ct=ct, ps=ps: e.matmul(
                            ps[:, 0:TT], lhsT=w[:, k, ct * 128:(ct + 1) * 128], rhs=h[:, k, :],
                            start=(k == 0), stop=(k == KD - 1)), reads=[Bw, bh], writes=[bp])
                    if ct < 8:
                        if not sample:
                            P.op("act", lambda e, ct=ct, ps=ps: e.activation(out=xc[:, ct, 3:3 + TT], in_=ps[:, 0:TT],
                                                                             func=AF.Copy), reads=[bp], writes=[(Bxc, ct)])
                            taps = [xc[:, ct, jj:jj + TT] for jj in range(4)]
                            accv = acc[:]; csv = cs[:]
                        else:
                            P.op("act", lambda e, ct=ct, ps=ps: e.activation(
                                out=xcv[:, ct, :, 3:35], in_=ps[:, 0:TT].rearrange("p (b w) -> p b w", w=32),
                                func=AF.Copy), reads=[bp], writes=[(Bxc, ct)])
                            taps = [xcv[:, ct, :, jj:jj + 32] for jj in range(4)]
                            accv = acc[:].rearrange("p (b w) -> p b w", w=32)
                            csv = cs[:].rearrange("p (b w) -> p b w", w=32)
                        P.op("dve", lambda e, ct=ct, taps=taps, accv=accv: e.tensor_scalar(
                            out=accv, in0=taps[0], scalar1=wc[:, ct, 0:1], scalar2=None, op0=ALU.mult),
                            reads=[(Bxc, ct)] + RK, writes=[Bacc])
                        for jj in range(1, 4):
                            P.op("dve", lambda e, ct=ct, jj=jj, taps=taps, accv=accv: e.scalar_tensor_tensor(
                                out=accv, in0=taps[jj], scalar=wc[:, ct, jj:jj + 1], in1=accv, op0=ALU.mult, op1=ALU.add),
                                reads=[(Bxc, ct)] + RK, writes=[Bacc])
                        if ct < 4:
                            P.op("act", lambda e: e.activation(out=cs[:], in_=acc[:], func=AF.Silu), reads=[Bacc], writes=[Bcs])
                            P.op("act", lambda e: e.activation(out=csq[:], in_=cs[:], func=AF.Square), reads=[Bcs], writes=[Bcsq])
                            pn, bn = bank()
                            P.op("pe", lambda e, pn=pn: e.matmul(pn[:, 0:TT], lhsT=onesb[:], rhs=csq[:], start=True, stop=True),
                                 reads=[Bcsq] + RC, writes=[bn])
                            P.op("act", lambda e, pn=pn: e.activation(out=rn[:], in_=pn[:, 0:TT], func=AF.Sqrt, bias=EPS),
                                 reads=[bn], writes=[Brn])
                            P.op("dve", lambda e: e.reciprocal(out=rn[:], in_=rn[:]), reads=[Brn], writes=[Brn])
                            dst, Bd, sc = (qT, BqT, 128 ** -0.5) if ct < 2 else (kT, BkT, 1.0)
                            P.op("dve", lambda e, dst=dst, ct=ct, sc=sc: e.scalar_tensor_tensor(
                                out=dst[:, ct % 2, :], in0=cs[:], scalar=sc, in1=rn[:], op0=ALU.mult, op1=ALU.mult),
                                reads=[Bcs, Brn], writes=[(Bd, ct % 2)])
                        else:
                            P.op("act", lambda e, ct=ct: e.activation(out=vT[:, ct - 4, :], in_=acc[:], func=AF.Silu),
                                 reads=[Bacc], writes=[(BvT, ct - 4)])
                    else:
                        P.op("act", lambda e, ct=ct, ps=ps: e.activation(out=szT[:, ct - 8, :], in_=ps[:, 0:TT], func=AF.Silu),
                             reads=[bp], writes=[(BszT, ct - 8)])
                if not sample:
                    if t == NPT - 1:
                        P.op("sp", lambda e: e.dma_start(out=o_pconv[j], in_=xc[:, :, TT:TT + 3]),
                             reads=[Bxc], writes=[B_out], dma=True)
                    P.op("pool", lambda e: e.tensor_copy(out=xc[:, :, 0:3], in_=xc[:, :, TT:TT + 3]),
                         reads=[Bxc], writes=[Bxc])
                else:
                    P.op("sp", lambda e, b0=b0: e.dma_start(out=o_sconv[j][:, :, b0:b0 + nseq, :], in_=xcv[:, :, :, 32:35]),
                         reads=[Bxc], writes=[B_out], dma=True)
                CH = 128 if not sample else DEC_S
                for ci in range(TT // CH):
                    col0 = ci * CH
                    pb_, bb_ = bank()
                    for k in range(KD):
                        P.op("pe", lambda e, k=k, pb_=pb_, col0=col0, CH=CH: e.matmul(
                            pb_[0:CH, 0:8], lhsT=h[:, k, col0:col0 + CH], rhs=w[:, k, 1536:1544],
                            start=(k == 0), stop=(k == KD - 1)), reads=[Bw, bh], writes=[bb_])
                    P.op("act", lambda e, pb_=pb_, CH=CH: e.activation(out=bet[0:CH, :], in_=pb_[0:CH, 0:4], func=AF.Sigmoid),
                         reads=[bb_], writes=[Bbet])
                    P.op("dve", lambda e, CH=CH: e.tensor_scalar(out=nbet[0:CH, :], in0=bet[0:CH, :], scalar1=-1.0,
                                                                scalar2=None, op0=ALU.mult), reads=[Bbet], writes=[Bbet])
                    P.op("dve", lambda e, pb_=pb_, CH=CH: e.tensor_tensor(out=gt[0:CH, :], in0=pb_[0:CH, 4:8],
                                                                         in1=dtb[0:CH, :], op=ALU.add),
                         reads=[bb_] + RK, writes=[Bg])
                    P.op("act", lambda e, CH=CH: e.activation(out=gt[0:CH, :], in_=gt[0:CH, :], func=AF.Exp), reads=[Bg], writes=[Bg])
                    P.op("act", lambda e, CH=CH: e.activation(out=gt[0:CH, :], in_=gt[0:CH, :], func=AF.Ln, bias=1.0),
                         reads=[Bg], writes=[Bg])
                    P.op("dve", lambda e, CH=CH: e.tensor_tensor(out=gt[0:CH, :], in0=gt[0:CH, :], in1=negA[0:CH, :], op=ALU.mult),
                         reads=[Bg] + RK, writes=[Bg])
                    if sample:
                        b = (t - NPT) * nseq + ci
                        P.op("sp", lambda e, b=b: e.dma_start(out=S[:], in_=g_s0[j, b]), writes=[BS], dma=True)
                        P.op("act", lambda e: e.activation(out=Sb[:], in_=S[:], func=AF.Copy), reads=[BS], writes=[BSb])
                    dstate["on"] = (t == 0 and ci == 0 and layer == 0)
                    chunk(CH, col0, og, Bog, col0)
                    dstate["on"] = False
                    if sample:
                        P.op("sp", lambda e, b=b: e.dma_start(out=o_srec[j, b], in_=S[:]), reads=[BS], writes=[B_out], dma=True)
                    elif t == NPT - 1 and ci == TT // CH - 1:
                        P.op("sp", lambda e: e.dma_start(out=o_prec[j], in_=S[:]), reads=[BS], writes=[B_out], dma=True)
                P.op("sp", lambda e, og=og, c0=c0: e.dma_start(
                    out=o_loc.rearrange("(h p) n -> p h n", p=128)[:, :, c0:c0 + TT], in_=og[:]),
                    reads=[Bog], writes=[B_oloc], dma=True)
            P.barrier()
        allgather(o_loc, B_oloc, o_all, B_oall)
        dense_out(o_all, B_oall, 32, g_wout[j])

    def att_layer(j, layer, src, add, add_bias):
        NTB = TT // 64
        with ExitStack() as s2:
            nctx = NormCtx(s2)
            w = sb("a_w", [128, KD, 768], BF16, s2); Bw = Buf()
            load_w_bf16(w, a_wqkv[j], Bw)
            bqk = sb("a_bqk", [128, 4], F32, s2)
            bv = sb("a_bv", [64, 256], F32, s2)
            bias = sb("a_bias", [64, AH, BAND], F32, s2)
            Bk = Buf()
            P.op("sp", lambda e: e.dma_start(out=bqk[:], in_=a_bqk[j]), writes=[(Bk, 0)], dma=True)
            P.op("sp", lambda e: e.dma_start(out=bv[:], in_=a_bv[j]), writes=[(Bk, 1)], dma=True)
            P.op("sp", lambda e: e.dma_start(out=bias[:], in_=a_bias[j]), writes=[(Bk, 2)], dma=True)
            RK = [Bk]
            WK = 512 + TT
            kw = sb("a_kw", [128, AH, WK], BF16, s2); Bkw = Buf()
            vw = sb("a_vw", [64, WK // 64, AH, 128], BF16, s2); Bvw = Buf()
            qTt = sb("a_qT", [128, AH, TT], BF16, s2); BqT = Buf()
            kf = sb("a_kf", [128, AH, TT], F32, s2); Bkf = Buf()
            vf = sb("a_vf", [64, NTB, AH, 128], F32, s2); Bvf = Buf()
            Ssb = sb("a_S", [64, BAND], F32, s2); BSs = Buf()
            Eb = sb("a_E", [64, BAND], BF16, s2); BE = Buf()
            mx = sb("a_mx", [64, 1], F32, s2); Bmx = Buf()
            sm = sb("a_sm", [64, 1], F32, s2); Bsm = Buf()
            dg = sb("a_dg", [64, 64], BF16, s2); Bdg = Buf()
            PT = sb("a_PT", [64, 9, 64], BF16, s2); BPT = Buf()
            oT = [sb(f"a_oT{i}", [128, AH, TT], BF16, s2) for i in range(2)]; BoT = [Buf(), Buf()]
            P.op("pool", lambda e: e.memset(kw[:], 0.0), writes=[Bkw])
            P.op("pool", lambda e: e.memset(vw[:], 0.0), writes=[Bvw])

            def attend(hd, Q, qcol, kparts, vblocks, nkeys, ot, Bot, ocol, mask_upto):
                off = 0
                for (kap, n) in kparts:
                    ps, bp = bank()
                    P.op("pe", lambda e, ps=ps, kap=kap, n=n: e.matmul(ps[0:Q, 0:n], lhsT=qTt[:, hd, qcol:qcol + Q],
                                                                      rhs=kap, start=True, stop=True),
                         reads=[BqT, Bkw], writes=[bp])
                    P.op("dve", lambda e, ps=ps, n=n, off=off: e.scalar_tensor_tensor(
                        out=Ssb[0:Q, off:off + n], in0=ps[0:Q, 0:n], scalar=128 ** -0.5, in1=bias[0:Q, hd, off:off + n],
                        op0=ALU.mult, op1=ALU.add), reads=[bp] + RK, writes=[(BSs, off)])
                    off += n
                if mask_upto > 0:
                    P.op("pool", lambda e: e.memset(Ssb[0:Q, 0:mask_upto], NEG), reads=[BSs], writes=[BSs])
                P.op("dve", lambda e: e.reduce_max(out=mx[0:Q, :], in_=Ssb[0:Q, 0:nkeys], axis=AX.X), reads=[BSs], writes=[Bmx])
                P.op("dve", lambda e: e.tensor_scalar(out=mx[0:Q, :], in0=mx[0:Q, :], scalar1=-1.0, scalar2=None, op0=ALU.mult),
                     reads=[Bmx], writes=[Bmx])
                P.op("act", lambda e: e.activation(out=Eb[0:Q, 0:nkeys], in_=Ssb[0:Q, 0:nkeys], func=AF.Exp,
                                                   bias=mx[0:Q, 0:1], accum_out=sm[0:Q, 0:1]),
                     reads=[BSs, Bmx], writes=[BE, Bsm])
                P.op("dve", lambda e: e.reciprocal(out=sm[0:Q, :], in_=sm[0:Q, :]), reads=[Bsm], writes=[Bsm])
                P.op("dve", lambda e: e.tensor_scalar(out=dg[0:Q, 0:Q], in0=identb[0:Q, 0:Q], scalar1=sm[0:Q, 0:1],
                                                      scalar2=None, op0=ALU.mult), reads=[Bsm] + RC, writes=[Bdg])
                nb = (nkeys + 63) // 64
                pa, ba_ = bank()
                pb2, bb2 = bank()
                for bl in range(nb):
                    kb = min(64, nkeys - bl * 64)
                    tgt = pa[0:kb, bl * 64:bl * 64 + Q] if bl < 8 else pb2[0:kb, 0:Q]
                    P.op("pe", lambda e, bl=bl, kb=kb, tgt=tgt: e.matmul(tgt, lhsT=Eb[0:Q, bl * 64:bl * 64 + kb],
                                                                        rhs=dg[0:Q, 0:Q], start=True, stop=True),
                         reads=[BE, Bdg], writes=[ba_ if bl < 8 else bb2])
                n8 = min(nb, 8)
                P.op("act", lambda e, pa=pa, n8=n8: e.activation(
                    out=PT[:, 0:n8, 0:Q], in_=pa[0:64, 0:n8 * 64].rearrange("p (b q) -> p b q", q=64)[:, :, 0:Q],
                    func=AF.Copy), reads=[ba_], writes=[(BPT, 0)])
                if nb > 8:
                    kb = nkeys - 512
                    P.op("act", lambda e, pb2=pb2, kb=kb: e.activation(out=PT[0:kb, 8, 0:Q], in_=pb2[0:kb, 0:Q], func=AF.Copy),
                         reads=[bb2], writes=[(BPT, 1)])
                po, bo_ = bank()
                for bl in range(nb):
                    kb = min(64, nkeys - bl * 64)
                    P.op("pe", lambda e, bl=bl, kb=kb, po=po: e.matmul(po[:, 0:Q], lhsT=vblocks[bl][0:kb, :],
                                                                      rhs=PT[0:kb, bl, 0:Q], start=(bl == 0), stop=(bl == nb - 1)),
                         reads=[BPT, Bvw], writes=[bo_])
                P.op("act", lambda e, po=po: e.activation(out=ot[:, hd, ocol:ocol + Q], in_=po[:, 0:Q], func=AF.Copy),
                     reads=[bo_], writes=[(Bot, hd)])

            for t in range(NTILE):
                c0 = t * TT
                sample = t >= NPT
                h, bh = nctx.run(t, src, gm[:, layer, :], add=add, bias=add_bias)
                ot, Bot = oT[t % 2], BoT[t % 2]
                for ct in range(4):
                    ps, bp = bank()
                    for k in range(KD):
                        P.op("pe", lambda e, k=k, ct=ct, ps=ps: e.matmul(
                            ps[:, 0:TT], lhsT=w[:, k, ct * 128:(ct + 1) * 128], rhs=h[:, k, :],
                            start=(k == 0), stop=(k == KD - 1)), reads=[Bw, bh], writes=[bp])
                    if ct < 2:
                        P.op("act", lambda e, ct=ct, ps=ps: e.activation(out=qTt[:, ct, :], in_=ps[:, 0:TT], func=AF.Identity,
                                                                         bias=bqk[:, ct:ct + 1]), reads=[bp] + RK, writes=[(BqT, ct)])
                    else:
                        P.op("act", lambda e, ct=ct, ps=ps: e.activation(out=kf[:, ct - 2, :], in_=ps[:, 0:TT], func=AF.Identity,
                                                                         bias=bqk[:, ct:ct + 1]), reads=[bp] + RK, writes=[(Bkf, ct)])
                for bl in range(NTB):
                    ps, bp = bank()
                    for k in range(KD):
                        P.op("pe", lambda e, k=k, bl=bl, ps=ps: e.matmul(
                            ps[0:64, 0:256], lhsT=h[:, k, bl * 64:(bl + 1) * 64], rhs=w[:, k, 512:768],
                            start=(k == 0), stop=(k == KD - 1)), reads=[Bw, bh], writes=[bp])
                    P.op("dve", lambda e, bl=bl, ps=ps: e.tensor_tensor(out=vf[:, bl].rearrange("p a d -> p (a d)"),
                                                                       in0=ps[0:64, 0:256], in1=bv[:], op=ALU.add),
                         reads=[bp] + RK, writes=[(Bvf, bl)])
                if not sample:
                    if t > 0:
                        P.op("pool", lambda e: e.tensor_copy(out=kw[:, :, 0:512], in_=kw[:, :, TT:TT + 512]), reads=[Bkw], writes=[Bkw])
                        P.op("pool", lambda e: e.tensor_copy(out=vw[:, 0:8], in_=vw[:, NTB:NTB + 8]), reads=[Bvw], writes=[Bvw])
                    P.op("pool", lambda e: e.tensor_copy(out=kw[:, :, 512:512 + TT], in_=kf[:]), reads=[Bkf], writes=[Bkw])
                    P.op("pool", lambda e: e.tensor_copy(out=vw[:, 8:8 + NTB], in_=vf[:]), reads=[Bvf], writes=[Bvw])
                    tl = t - (NPT - 512 // TT)
                    if tl >= 0:
                        P.op("sp", lambda e, tl=tl: e.dma_start(out=o_pk[j][:, :, tl * TT:(tl + 1) * TT], in_=kf[:]),
                             reads=[Bkf], writes=[B_out], dma=True)
                        P.op("sp", lambda e, tl=tl: e.dma_start(out=o_pv[j][:, tl * NTB:(tl + 1) * NTB], in_=vf[:]),
                             reads=[Bvf], writes=[B_out], dma=True)
                    for qc in range(TT // 64):
                        gchunk = t * (TT // 64) + qc
                        kstart = qc * 64
                        mask_upto = max(0, 512 - 64 * gchunk)
                        for hd in range(AH):
                            n1 = min(512, BAND)
                            kparts = [(kw[:, hd, kstart:kstart + 512], 512), (kw[:, hd, kstart + 512:kstart + 576], 64)]
                            vbl = [vw[:, kstart // 64 + bl, hd, :] for bl in range(9)]
                            attend(hd, 64, qc * 64, kparts, vbl, BAND, ot, Bot, qc * 64, mask_upto)
                else:
                    nseq = TT // DEC_S
                    P.op("sp", lambda e, c0=c0: e.dma_start(out=o_sk[j][:, :, c0 - SEQ:c0 - SEQ + TT], in_=kf[:]),
                         reads=[Bkf], writes=[B_out], dma=True)
                    for si in range(nseq):
                        b = (t - NPT) * nseq + si
                        P.op("sp", lambda e, b=b, si=si: e.dma_start(
                            out=o_sv[j, b], in_=vf[(si % 2) * 32:(si % 2) * 32 + 32, si // 2]),
                            reads=[Bvf], writes=[B_out], dma=True)
                        P.op("pool", lambda e, b=b: e.dma_start(out=kw[:, :, 0:512], in_=a_ck[j, b]), writes=[Bkw], dma=True)
                        P.op("pool", lambda e, b=b: e.dma_start(out=vw[:, 0:8], in_=a_cv[j, b]), writes=[Bvw], dma=True)
                        P.op("pool", lambda e, si=si: e.tensor_copy(out=kw[:, :, 512:544], in_=kf[:, :, si * 32:si * 32 + 32]),
                             reads=[Bkf], writes=[Bkw])
                        P.op("pool", lambda e, si=si: e.dma_start(out=vw[0:32, 8], in_=vf[(si % 2) * 32:(si % 2) * 32 + 32, si // 2]),
                             reads=[Bvf], writes=[Bvw], dma=True)
                        for hd in range(AH):
                            kparts = [(kw[:, hd, 0:512], 512), (kw[:, hd, 512:544], 32)]
                            vbl = [vw[:, bl, hd, :] for bl in range(9)]
                            attend(hd, 32, si * 32, kparts, vbl, 544, ot, Bot, si * 32, 0)
                P.op("sp", lambda e, ot=ot, c0=c0: e.dma_start(
                    out=oa_loc.rearrange("(h p) n -> p h n", p=128)[:, :, c0:c0 + TT], in_=ot[:]),
                    reads=[Bot], writes=[B_oaloc], dma=True)
            P.barrier()
        allgather(oa_loc, B_oaloc, oa_all, B_oaall)
        dense_out(oa_all, B_oaall, 16, a_wo[j])

    def dump(name, src_ap, Bsrc, dt):
        d = dout(name, list(src_ap.shape), dt)
        P.op("sp", lambda e: e.dma_start(out=d, in_=src_ap), reads=[Bsrc], writes=[B_out], dma=True)

    src = xT_in
    add = None
    addb = None
    for layer in range(depth):
        j = layer // 2
        if layer % 2 == 0:
            gdn_layer(j, layer, src, add, addb)
            if dbg and layer == 0:
                dump("dbg_o", o_all, B_oall, BF16)
                dump("dbg_y", y_all, B_yall, F32)
            ffn_layer(layer, src, None)
            if dbg and layer == 0:
                dump("dbg_x1", xT, B_xT, F32)
                dump("dbg_f", y_all, B_yall, F32)
        else:
            att_layer(j, layer, src, add, addb)
            ffn_layer(layer, xT, bo_t[:, j, :])
        src = xT
        add = y_all
        addb = None
    with ExitStack() as s2:
        nctx = NormCtx(s2)
        hf = [sb(f"fin_h{i}", [128, KD, TT], F32, s2) for i in range(2)]; Bhf = [Buf(), Buf()]
        for t in range(NTILE):
            c0 = t * TT
            h, bh = nctx.run(t, xT, gfin[:], add=y_all, h_f32=(hf[t % 2], Bhf[t % 2]))
            P.op("sp", lambda e, h=h, c0=c0: e.dma_start(out=xv(yT)[:, :, c0:c0 + TT], in_=h[:]),
                 reads=[bh], writes=[B_out], dma=True)
        P.barrier()
    P.barrier()
    P.emit()
    st.close()
    return nc


_CACHE = {}


def _host_inputs(SEQ, inp):
    f = np.float32
    NT = SEQ + DEC_B * DEC_S
    x = np.concatenate([inp["x_prompt"][0], inp["x_sample"].reshape(DEC_B * DEC_S, D)], 0)
    common = {}
    common["xT_in"] = np.ascontiguousarray(x.T).astype(f)
    jj, ii = np.meshgrid(np.arange(128), np.arange(128), indexing="ij")
    common["cst_f"] = np.ascontiguousarray(np.stack([(ii == jj), (ii >= jj), (ii > jj)], 1).astype(f))
    common["ones_f"] = np.ones((128, 128), f)
    pk = lambda a: np.ascontiguousarray(a.reshape(a.shape[0], KD, 128).transpose(2, 0, 1))
    common["gam_mix"] = pk(inp["norm_mix"])
    common["gam_ffn"] = pk(inp["norm_ffn"])
    common["gam_fin"] = np.ascontiguousarray(inp["norm_final"].reshape(KD, 128).T)
    common["a_bo"] = pk(inp["att_b_o"])
    table = inp["att_rel_bias"]
    q = np.arange(64)[:, None]
    ko = np.arange(BAND)[None, :]
    ridx = np.clip(q + 512 - ko, -256, 256) + 256
    bias_full = table[:, :, ridx]
    maps = []
    for c in range(NCORES):
        m = dict(common)
        qk = np.arange(2 * c * 128, (2 * c + 2) * 128)
        vv = np.arange(4 * c * 128, (4 * c + 4) * 128)
        hh = np.arange(4 * c, 4 * c + 4)
        cols = np.concatenate([qk, 2048 + qk, 4096 + vv, 8192 + vv, 12288 + hh, 12320 + hh])
        m["g_win"] = np.ascontiguousarray(inp["gdn_w_in"][:, :, cols])
        ch = np.concatenate([qk, 2048 + qk, 4096 + vv]).reshape(8, 128)
        m["g_wconv"] = np.ascontiguousarray(inp["gdn_w_conv"][:, :, ch].transpose(0, 3, 2, 1))
        m["g_alog"] = np.ascontiguousarray(np.broadcast_to(inp["gdn_a_log"][:, None, hh], (2, 128, GH)))
        m["g_dtb"] = np.ascontiguousarray(np.broadcast_to(inp["gdn_dt_bias"][:, None, hh], (2, 128, GH)))
        m["g_wnorm"] = np.ascontiguousarray(inp["gdn_w_norm"][:, :, None])
        m["g_wout"] = np.ascontiguousarray(inp["gdn_w_out"][:, :, 256 * c:256 * c + 256])
        m["g_s0"] = np.ascontiguousarray(inp["state_gdn_rec"][:, :, hh].transpose(0, 1, 3, 2, 4))
        m["g_c0"] = np.ascontiguousarray(inp["state_gdn_conv"][:, :, :, ch].transpose(0, 4, 3, 1, 2))
        ac = np.arange(256 * c, 256 * c + 256)
        m["a_wqkv"] = np.ascontiguousarray(inp["att_w_qkv"][:, :, np.concatenate([ac, 2048 + ac, 4096 + ac])])
        bq = inp["att_b_qkv"]
        m["a_bqk"] = np.ascontiguousarray(np.stack(
            [bq[:, 256 * c:256 * c + 128], bq[:, 256 * c + 128:256 * c + 256],
             bq[:, 2048 + 256 * c:2048 + 256 * c + 128], bq[:, 2048 + 256 * c + 128:2048 + 256 * c + 256]], 2))
        m["a_bv"] = np.ascontiguousarray(np.broadcast_to(bq[:, None, 4096 + 256 * c:4096 + 256 * c + 256], (2, 64, 256)))
        m["a_bias"] = np.ascontiguousarray(bias_full[:, 2 * c:2 * c + 2].transpose(0, 2, 1, 3))
        m["a_wo"] = np.ascontiguousarray(inp["att_w_o"][:, :, 256 * c:256 * c + 256])
        m["a_ck"] = np.ascontiguousarray(inp["cache_att_k"][:, :, :, 2 * c:2 * c + 2].transpose(0, 1, 4, 3, 2))
        cv = inp["cache_att_v"][:, :, :, 2 * c:2 * c + 2]
        m["a_cv"] = np.ascontiguousarray(cv.reshape(2, DEC_B, 8, 64, AH, 128).transpose(0, 1, 3, 2, 4, 5))
        m["f_wg"] = np.ascontiguousarray(inp["ffn_w_gate"][:, :, FFC * c:FFC * (c + 1)])
        m["f_wu"] = np.ascontiguousarray(inp["ffn_w_up"][:, :, FFC * c:FFC * (c + 1)])
        m["f_wd"] = np.ascontiguousarray(inp["ffn_w_down"][:, :, 256 * c:256 * c + 256])
        maps.append({k: np.asarray(v, dtype=f) for k, v in m.items()})
    return maps


def _assemble(SEQ, res):
    f = np.float32
    y = res[0]["yT"].T
    y_prompt = np.ascontiguousarray(y[:SEQ]).reshape(1, SEQ, D).astype(f)
    y_sample = np.ascontiguousarray(y[SEQ:]).reshape(DEC_B, DEC_S, D).astype(f)
    p_rec = np.zeros((2, 1, 32, 128, 128), f)
    p_conv = np.zeros((2, 1, 3, 8192), f)
    s_rec = np.zeros((2, DEC_B, 32, 128, 128), f)
    s_conv = np.zeros((2, DEC_B, 3, 8192), f)
    p_k = np.zeros((2, 1, 512, 16, 128), f)
    p_v = np.zeros((2, 1, 512, 16, 128), f)
    s_k = np.zeros((2, DEC_B, DEC_S, 16, 128), f)
    s_v = np.zeros((2, DEC_B, DEC_S, 16, 128), f)
    for c in range(NCORES):
        r = res[c]
        qk = np.arange(2 * c * 128, (2 * c + 2) * 128)
        vv = np.arange(4 * c * 128, (4 * c + 4) * 128)
        ch = np.concatenate([qk, 2048 + qk, 4096 + vv]).reshape(8, 128)
        p_rec[:, 0, 4 * c:4 * c + 4] = r["o_prec"].transpose(0, 2, 1, 3)
        s_rec[:, :, 4 * c:4 * c + 4] = r["o_srec"].transpose(0, 1, 3, 2, 4)
        p_conv[:, 0][:, :, ch] = r["o_pconv"].transpose(0, 3, 2, 1)
        s_conv[:, :, :, ch] = r["o_sconv"].transpose(0, 3, 4, 2, 1)
        p_k[:, 0, :, 2 * c:2 * c + 2] = r["o_pk"].transpose(0, 3, 2, 1)
        p_v[:, 0, :, 2 * c:2 * c + 2] = r["o_pv"].transpose(0, 2, 1, 3, 4).reshape(2, 512, AH, 128)
        s_k[:, :, :, 2 * c:2 * c + 2] = r["o_sk"].transpose(0, 3, 2, 1).reshape(2, DEC_B, DEC_S, AH, 128)
        s_v[:, :, :, 2 * c:2 * c + 2] = r["o_sv"]
    return (y_prompt, y_sample, p_rec, p_conv, p_k, p_v, s_rec, s_conv, s_k, s_v)


def kernel(**inputs):
    inp = {k: np.asarray(v) for k, v in inputs.items()}
    SEQ = inp["x_prompt"].shape[1]
    if SEQ not in _CACHE:
        _CACHE[SEQ] = build(SEQ)
    nc = _CACHE[SEQ]
    maps = _host_inputs(SEQ, inp)
    out = run_bass_kernel_spmd(nc, maps, core_ids=list(range(NCORES)))
    return _assemble(SEQ, out.results)


def _debug_run(inp, depth, dbg=True):
    SEQ = inp["x_prompt"].shape[1]
    nc = build(SEQ, depth=depth, dbg=dbg)
    maps = _host_inputs(SEQ, inp)
    out = run_bass_kernel_spmd(nc, maps, core_ids=list(range(NCORES)))
    return out.results, _assemble(SEQ, out.results)
```

```python
import numpy as np
from contextlib import ExitStack
import ml_dtypes
import concourse.bass as bass
import concourse.mybir as mybir
from concourse.bass_utils import run_bass_kernel_spmd

F32 = mybir.dt.float32
BF16 = mybir.dt.bfloat16
AF = mybir.ActivationFunctionType
ALU = mybir.AluOpType
AX = mybir.AxisListType

NCORES = 8
D = 2048
KD = D // 128
DEPTH = 4
DEC_B = 16
DEC_S = 32
EPS = 1e-6
GH = 4
GQ = 2
GCOLS = 1544
AH = 2
DFF = 5632
FFC = DFF // NCORES
TT = 256
BAND = 576
NEG = -30000.0

ENGS = ("pe", "act", "dve", "pool", "sp")
DMA_RING = 8


class Buf:
    def __init__(self, name="", excl=False):
        self.name = name
        self.excl = excl
        self.whole_w = None
        self.whole_r = []
        self.parts = {}


class _Rec:
    def __init__(self):
        self.call = None

    def __getattr__(self, name):
        def f(*a, **kw):
            self.call = (name, a, kw)
            return self
        return f


class Prog:
    def __init__(self, nc):
        self.nc = nc
        self.stream = {e: [] for e in ENGS}
        self.cnt = {e: 0 for e in ENGS}
        self.sem = {}
        self.dsem = {}
        self.dcnt = {}
        self.duse = {}
        self.seen = {e: {} for e in ENGS}
        self.alltoks = []

    def open(self, stack):
        for e in ENGS:
            self.sem[e] = stack.enter_context(self.nc.semaphore(f"c_{e}"))
        for e in ("sp", "act", "pool", "cc"):
            self.dsem[e] = [stack.enter_context(self.nc.semaphore(f"d_{e}{i}")) for i in range(DMA_RING)]
            self.dcnt[e] = 0
            for i in range(DMA_RING):
                self.duse[(e, i)] = 0

    def _wait(self, e, tok):
        if tok is None:
            return
        kind, te, slot, val = tok
        key = (kind, te, slot)
        if self.seen[e].get(key, 0) >= val:
            return
        self.seen[e][key] = val
        sem = self.sem[te] if kind == "c" else self.dsem[te][slot]
        self.stream[e].append(("wait", sem, val))

    @staticmethod
    def _bk(x):
        return x if isinstance(x, tuple) else (x, None)

    def _deps(self, reads, writes):
        toks = []
        for r in reads:
            b, k = self._bk(r)
            toks.append(b.whole_w)
            if k is None:
                for p in b.parts.values():
                    toks.append(p[0])
            elif k in b.parts:
                toks.append(b.parts[k][0])
        for w in writes:
            b, k = self._bk(w)
            toks.append(b.whole_w)
            toks.extend(b.whole_r)
            if k is None:
                for p in b.parts.values():
                    toks.append(p[0])
                    toks.extend(p[1])
            elif k in b.parts:
                toks.append(b.parts[k][0])
                toks.extend(b.parts[k][1])
        return toks

    def _commit(self, tok, reads, writes):
        for r in reads:
            b, k = self._bk(r)
            if k is None:
                b.whole_r.append(tok)
            else:
                b.parts.setdefault(k, [None, []])[1].append(tok)
        for w in writes:
            b, k = self._bk(w)
            if k is None:
                b.whole_w = tok
                b.whole_r = []
                b.parts = {}
            else:
                b.parts[k] = [tok, []]

    serial = False

    def op(self, e, fn, reads=(), writes=(), dma=False, ring=None):
        if self.serial:
            for e2 in ENGS:
                if self.cnt[e2] > 0:
                    self._wait(e, ("c", e2, 0, self.cnt[e2]))
            for (r_, slot_), uses_ in self.duse.items():
                if uses_ > 0:
                    self._wait(e, ("d", r_, slot_, (1 if r_ == "cc" else 16) * uses_))
        rec = _Rec()
        fn(rec)
        fn = rec.call
        ex = [r for r in reads if self._bk(r)[0].excl]
        if ex:
            reads = [r for r in reads if not self._bk(r)[0].excl]
            writes = list(writes) + ex
        toks = self._deps(reads, writes)
        if dma:
            r = ring or e
            inc = 1 if r == "cc" else 16
            slot = self.dcnt[r] % DMA_RING
            self.dcnt[r] += 1
            uses = self.duse[(r, slot)]
            if uses > 0:
                self._wait(e, ("d", r, slot, inc * uses))
            self.duse[(r, slot)] = uses + 1
            tok = ("d", r, slot, inc * (uses + 1))
            for t in toks:
                self._wait(e, t)
            self.stream[e].append(("op", fn, self.dsem[r][slot], inc))
        else:
            for t in toks:
                if t is not None and e == "pe" and t[0] == "c" and t[1] == "pe":
                    continue
                self._wait(e, t)
            self.cnt[e] += 1
            tok = ("c", e, 0, self.cnt[e])
            self.stream[e].append(("op", fn, self.sem[e], 1))
        self._commit(tok, reads, writes)
        return tok

    def barrier(self):
        toks = [("c", e, 0, self.cnt[e]) for e in ENGS if self.cnt[e] > 0]
        for (r, slot), uses in self.duse.items():
            if uses > 0:
                toks.append(("d", r, slot, (1 if r == "cc" else 16) * uses))
        for e in ENGS:
            for t in toks:
                if t[0] == "c" and t[1] == e:
                    continue
                self._wait(e, t)

    def emit(self):
        nc = self.nc
        with nc.Block() as block:
            def mk(e):
                def body(engine):
                    for it in self.stream[e]:
                        if it[0] == "wait":
                            engine.wait_ge(it[1], it[2])
                        else:
                            nm, a, kw = it[1]
                            getattr(engine, nm)(*a, **kw).then_inc(it[2], it[3])
                return body
            block.tensor(mk("pe"))
            block.scalar(mk("act"))
            block.vector(mk("dve"))
            block.gpsimd(mk("pool"))
            block.sync(mk("sp"))


SERIAL = False


def build(SEQ, depth=DEPTH, dbg=False):
    NT = SEQ + DEC_B * DEC_S
    NTILE = NT // TT
    NPT = SEQ // TT
    assert SEQ % TT == 0 and SEQ >= 512
    nc = bass.Bass("TRN2", target_bir_lowering=False)
    st = ExitStack()
    P = Prog(nc)
    P.serial = SERIAL
    P.open(st)

    def din(name, shape, dt=F32):
        return nc.dram_tensor(name, list(shape), dt, kind="ExternalInput").ap()

    def dout(name, shape, dt=F32):
        return nc.dram_tensor(name, list(shape), dt, kind="ExternalOutput").ap()

    def dint(name, shape, dt):
        return nc.dram_tensor(name, list(shape), dt, kind="Internal").ap()

    xT_in = din("xT_in", [D, NT])
    cst_f = din("cst_f", [128, 3, 128])
    ones_f_d = din("ones_f", [128, 128])
    gam_mix = din("gam_mix", [128, DEPTH, KD])
    gam_ffn = din("gam_ffn", [128, DEPTH, KD])
    gam_fin = din("gam_fin", [128, KD])
    g_win = din("g_win", [2, D, GCOLS])
    g_wconv = din("g_wconv", [2, 128, 8, 4])
    g_alog = din("g_alog", [2, 128, GH])
    g_dtb = din("g_dtb", [2, 128, GH])
    g_wnorm = din("g_wnorm", [2, 128, 1])
    g_wout = din("g_wout", [2, 4096, 256])
    g_s0 = din("g_s0", [2, DEC_B, 128, GH, 128])
    g_c0 = din("g_c0", [2, 128, 8, DEC_B, 3])
    a_wqkv = din("a_wqkv", [2, D, 768])
    a_bqk = din("a_bqk", [2, 128, 4])
    a_bv = din("a_bv", [2, 64, 256])
    a_bias = din("a_bias", [2, 64, AH, BAND])
    a_wo = din("a_wo", [2, D, 256])
    a_bo = din("a_bo", [128, 2, KD])
    a_ck = din("a_ck", [2, DEC_B, 128, AH, 512])
    a_cv = din("a_cv", [2, DEC_B, 64, 8, AH, 128])
    f_wg = din("f_wg", [DEPTH, D, FFC])
    f_wu = din("f_wu", [DEPTH, D, FFC])
    f_wd = din("f_wd", [DEPTH, DFF, 256])
    yT = dout("yT", [D, NT])
    o_prec = dout("o_prec", [2, 128, GH, 128])
    o_pconv = dout("o_pconv", [2, 128, 8, 3])
    o_srec = dout("o_srec", [2, DEC_B, 128, GH, 128])
    o_sconv = dout("o_sconv", [2, 128, 8, DEC_B, 3])
    o_pk = dout("o_pk", [2, 128, AH, 512])
    o_pv = dout("o_pv", [2, 64, 8, AH, 128])
    o_sk = dout("o_sk", [2, 128, AH, DEC_B * DEC_S])
    o_sv = dout("o_sv", [2, DEC_B, DEC_S, AH, 128])
    xT = dint("xT_res", [D, NT], F32)
    o_loc = dint("o_loc", [512, NT], BF16)
    o_all = dint("o_all", [4096, NT], BF16)
    oa_loc = dint("oa_loc", [256, NT], BF16)
    oa_all = dint("oa_all", [2048, NT], BF16)
    y_loc = dint("y_loc", [256, NT], F32)
    y_all = dint("y_all", [D, NT], F32)
    act_loc = dint("act_loc", [FFC, NT], BF16)
    act_all = dint("act_all", [DFF, NT], BF16)
    B_xT = Buf("xT"); B_oloc = Buf(); B_oall = Buf(); B_oaloc = Buf(); B_oaall = Buf()
    B_yloc = Buf(); B_yall = Buf(); B_aloc = Buf(); B_aall = Buf()
    B_out = Buf("outputs")
    GROUPS = [list(range(NCORES))]

    uid = [0]

    def sb(name, shape, dt, stack):
        uid[0] += 1
        return stack.enter_context(nc.sbuf_tensor(f"s{uid[0]}_{name}", list(shape), dt))

    banks = [st.enter_context(nc.psum_tensor(f"ps{i}", [128, 512], F32)) for i in range(8)]
    bbufs = [Buf(f"ps{i}", excl=True) for i in range(8)]
    bstate = [0]

    def bank():
        i = bstate[0] % 8
        bstate[0] += 1
        return banks[i], bbufs[i]

    cf = sb("cf", [128, 3, 128], F32, st); B_c = Buf("const")
    onesf = sb("onesf", [128, 128], F32, st)
    onesb = sb("onesb", [128, 128], BF16, st)
    identb = sb("identb", [128, 128], BF16, st)
    gm = sb("gm", [128, DEPTH, KD], F32, st)
    gf = sb("gf", [128, DEPTH, KD], F32, st)
    gfin = sb("gfin", [128, KD], F32, st)
    bo_t = sb("bo_t", [128, 2, KD], F32, st)
    P.op("sp", lambda e: e.dma_start(out=cf[:], in_=cst_f), writes=[(B_c, 0)], dma=True)
    P.op("sp", lambda e: e.dma_start(out=onesf[:], in_=ones_f_d), writes=[(B_c, 1)], dma=True)
    P.op("sp", lambda e: e.dma_start(out=gm[:], in_=gam_mix), writes=[(B_c, 2)], dma=True)
    P.op("sp", lambda e: e.dma_start(out=gf[:], in_=gam_ffn), writes=[(B_c, 3)], dma=True)
    P.op("sp", lambda e: e.dma_start(out=gfin[:], in_=gam_fin), writes=[(B_c, 4)], dma=True)
    P.op("sp", lambda e: e.dma_start(out=bo_t[:], in_=a_bo), writes=[(B_c, 5)], dma=True)
    P.op("dve", lambda e: e.tensor_copy(out=onesb[:], in_=onesf[:]), reads=[(B_c, 1)], writes=[(B_c, 6)])
    P.op("dve", lambda e: e.tensor_copy(out=identb[:], in_=cf[:, 0, :]), reads=[(B_c, 0)], writes=[(B_c, 7)])
    ident_f = cf[:, 0, :]
    mask_i = cf[:, 1, :]
    mask_s = cf[:, 2, :]
    P.barrier()
    RC = [B_c]

    xv = lambda ap: ap.rearrange("(k p) n -> p k n", p=128)
    dstate = {"on": False}

    def ddump(name, ap, B, dt=F32):
        if not (dbg and dstate["on"]):
            return
        d = dout("dd_" + name, list(ap.shape), dt)
        P.op("sp", lambda e: e.dma_start(out=d, in_=ap), reads=[B], writes=[B_out], dma=True)

    class NormCtx:
        def __init__(self, stack):
            self.x = sb("n_x", [128, KD, TT], F32, stack); self.Bx = Buf()
            self.y = sb("n_y", [128, KD, TT], F32, stack); self.By = Buf()
            self.sq = sb("n_sq", [128, KD, TT], BF16, stack); self.Bsq = Buf()
            self.rs = sb("n_rs", [128, TT], F32, stack); self.Brs = Buf()
            self.h = [sb(f"n_h{i}", [128, KD, TT], BF16, stack) for i in range(2)]
            self.Bh = [Buf(), Buf()]
            self.i = 0

        def run(self, t, src, gamma, add=None, bias=None, h_f32=None):
            c0 = t * TT
            x, y, sq, rs = self.x, self.y, self.sq, self.rs
            P.op("sp", lambda e: e.dma_start(out=x[:], in_=xv(src)[:, :, c0:c0 + TT]),
                 reads=[(B_xT, t)], writes=[self.Bx], dma=True)
            if add is not None:
                P.op("act", lambda e: e.dma_start(out=y[:], in_=xv(add)[:, :, c0:c0 + TT]),
                     reads=[B_yall], writes=[self.By], dma=True)
                if bias is None:
                    P.op("pool", lambda e: e.tensor_tensor(out=x[:], in0=x[:], in1=y[:], op=ALU.add),
                         reads=[self.By], writes=[self.Bx])
                else:
                    for k in range(KD):
                        P.op("dve", lambda e, k=k: e.scalar_tensor_tensor(
                            out=x[:, k, :], in0=y[:, k, :], scalar=bias[:, k:k + 1], in1=x[:, k, :],
                            op0=ALU.add, op1=ALU.add), reads=[self.By] + RC, writes=[self.Bx])
                P.op("sp", lambda e: e.dma_start(out=xv(xT)[:, :, c0:c0 + TT], in_=x[:]),
                     reads=[self.Bx], writes=[(B_xT, t)], dma=True)
            P.op("act", lambda e: e.activation(out=sq[:], in_=x[:], func=AF.Square),
                 reads=[self.Bx], writes=[self.Bsq])
            ps, bp = bank()
            for k in range(KD):
                P.op("pe", lambda e, k=k: e.matmul(ps[:, 0:TT], lhsT=onesb[:], rhs=sq[:, k, :],
                                                   start=(k == 0), stop=(k == KD - 1)),
                     reads=[self.Bsq] + RC, writes=[bp])
            P.op("act", lambda e: e.activation(out=rs[:], in_=ps[:, 0:TT], func=AF.Sqrt,
                                               scale=1.0 / D, bias=EPS), reads=[bp], writes=[self.Brs])
            P.op("dve", lambda e: e.reciprocal(out=rs[:], in_=rs[:]), reads=[self.Brs], writes=[self.Brs])
            if h_f32 is not None:
                h, bh = h_f32
            else:
                h, bh = self.h[self.i % 2], self.Bh[self.i % 2]
                self.i += 1
            for k in range(KD):
                eng = "dve"
                P.op(eng, lambda e, k=k: e.scalar_tensor_tensor(
                    out=h[:, k, :], in0=x[:, k, :], scalar=gamma[:, k:k + 1], in1=rs[:],
                    op0=ALU.mult, op1=ALU.mult), reads=[self.Bx, self.Brs] + RC, writes=[(bh, k)])
            return h, bh

    def load_w_bf16(dst, dram_ap, B):
        K = dst.shape[1]
        for k0 in range(0, K, 4):
            k1 = min(K, k0 + 4)
            P.op("pool", lambda e, k0=k0, k1=k1: e.dma_start(
                out=dst[:, k0:k1, :], in_=xv(dram_ap)[:, k0:k1, :]), writes=[(B, k0)], dma=True)

    def allgather(src, Bs, dst, Bd):
        P.op("pool", lambda e: e.collective_compute("AllGather", ALU.bypass, replica_groups=GROUPS,
                                                   ins=[src.opt()], outs=[dst.opt()]),
             reads=[Bs], writes=[Bd], dma=True, ring="cc")

    def dense_out(o_src, Bsrc, KT, w_dram, bias_cols=None):
        with ExitStack() as s2:
            w = sb("do_w", [128, KT, 256], BF16, s2); Bw = Buf()
            load_w_bf16(w, w_dram, Bw)
            ins = [sb(f"do_in{i}", [128, KT, TT], BF16, s2) for i in range(2)]; Bin = [Buf(), Buf()]
            yo = [sb(f"do_y{i}", [128, 2, TT], F32, s2) for i in range(2)]; Byo = [Buf(), Buf()]
            for t in range(NTILE):
                c0 = t * TT
                it, Bi = ins[t % 2], Bin[t % 2]
                yt, By = yo[t % 2], Byo[t % 2]
                P.op("sp", lambda e, it=it, c0=c0: e.dma_start(out=it[:], in_=xv(o_src)[:, :, c0:c0 + TT]),
                     reads=[Bsrc], writes=[Bi], dma=True)
                for ct in range(2):
                    ps, bp = bank()
                    for k in range(KT):
                        P.op("pe", lambda e, k=k, ct=ct, it=it, ps=ps: e.matmul(
                            ps[:, 0:TT], lhsT=w[:, k, ct * 128:(ct + 1) * 128], rhs=it[:, k, :],
                            start=(k == 0), stop=(k == KT - 1)), reads=[Bw, Bi], writes=[bp])
                    P.op("act", lambda e, ct=ct, yt=yt, ps=ps: e.activation(
                        out=yt[:, ct, :], in_=ps[:, 0:TT], func=AF.Copy), reads=[bp], writes=[(By, ct)])
                P.op("sp", lambda e, yt=yt, c0=c0: e.dma_start(
                    out=y_loc.rearrange("(c p) n -> p c n", p=128)[:, :, c0:c0 + TT], in_=yt[:]),
                    reads=[By], writes=[(B_yloc, t)], dma=True)
            P.barrier()
        allgather(y_loc, B_yloc, y_all, B_yall)

    def ffn_layer(layer, src, add_bias):
        with ExitStack() as s2:
            nctx = NormCtx(s2)
            wg = sb("f_wg", [128, KD, FFC], BF16, s2); Bwg = Buf()
            wu = sb("f_wu", [128, KD, FFC], BF16, s2); Bwu = Buf()
            load_w_bf16(wg, f_wg[layer], Bwg)
            load_w_bf16(wu, f_wu[layer], Bwu)
            sg = [sb(f"f_sg{i}", [128, TT], F32, s2) for i in range(2)]; Bsg = [Buf(), Buf()]
            at = [sb(f"f_at{i}", [128, 6, TT], BF16, s2) for i in range(2)]; Bat = [Buf(), Buf()]
            ftiles = [(i * 128, 128) for i in range(5)] + [(640, 64)]
            pend = nctx.run(0, src, gf[:, layer, :], add=y_all, bias=add_bias)
            for t in range(NTILE):
                c0 = t * TT
                h, bh = pend
                if t + 1 < NTILE:
                    pend = nctx.run(t + 1, src, gf[:, layer, :], add=y_all, bias=add_bias)
                a, Ba = at[t % 2], Bat[t % 2]
                for fi, (f0, fw) in enumerate(ftiles):
                    pg, bg = bank()
                    pu, bu = bank()
                    for k in range(KD):
                        P.op("pe", lambda e, k=k, pg=pg, f0=f0, fw=fw: e.matmul(
                            pg[0:fw, 0:TT], lhsT=wg[:, k, f0:f0 + fw], rhs=h[:, k, :],
                            start=(k == 0), stop=(k == KD - 1)), reads=[Bwg, bh], writes=[bg])
                    for k in range(KD):
                        P.op("pe", lambda e, k=k, pu=pu, f0=f0, fw=fw: e.matmul(
                            pu[0:fw, 0:TT], lhsT=wu[:, k, f0:f0 + fw], rhs=h[:, k, :],
                            start=(k == 0), stop=(k == KD - 1)), reads=[Bwu, bh], writes=[bu])
                    s, Bs = sg[fi % 2], Bsg[fi % 2]
                    P.op("act", lambda e, s=s, pg=pg, fw=fw: e.activation(
                        out=s[0:fw, :], in_=pg[0:fw, 0:TT], func=AF.Silu), reads=[bg], writes=[Bs])
                    P.op("dve", lambda e, s=s, pu=pu, fw=fw, a=a, fi=fi: e.tensor_tensor(
                        out=a[0:fw, fi, :], in0=pu[0:fw, 0:TT], in1=s[0:fw, :], op=ALU.mult),
                        reads=[bu, Bs], writes=[(Ba, fi)])
                P.op("sp", lambda e, a=a, c0=c0: e.dma_start(
                    out=act_loc[0:640, :].rearrange("(c p) n -> p c n", p=128)[:, :, c0:c0 + TT],
                    in_=a[:, 0:5, :]), reads=[Ba], writes=[(B_aloc, 2 * t)], dma=True)
                P.op("sp", lambda e, a=a, c0=c0: e.dma_start(
                    out=act_loc[640:704, c0:c0 + TT], in_=a[0:64, 5, :]), reads=[Ba], writes=[(B_aloc, 2 * t + 1)], dma=True)
            P.barrier()
        allgather(act_loc, B_aloc, act_all, B_aall)
        dense_out(act_all, B_aall, DFF // 128, f_wd[layer])

    def gdn_layer(j, layer, src, add, add_bias):
        with ExitStack() as s2:
            nctx = NormCtx(s2)
            w = sb("g_w", [128, KD, GCOLS], BF16, s2); Bw = Buf()
            load_w_bf16(w, g_win[j], Bw)
            wc = sb("g_wc", [128, 8, 4], F32, s2)
            alog = sb("g_alog", [128, GH], F32, s2)
            dtb = sb("g_dtb", [128, GH], F32, s2)
            wnm = sb("g_wnm", [128, 1], F32, s2)
            Bk = Buf()
            P.op("sp", lambda e: e.dma_start(out=wc[:], in_=g_wconv[j]), writes=[(Bk, 0)], dma=True)
            P.op("sp", lambda e: e.dma_start(out=alog[:], in_=g_alog[j]), writes=[(Bk, 1)], dma=True)
            P.op("sp", lambda e: e.dma_start(out=dtb[:], in_=g_dtb[j]), writes=[(Bk, 2)], dma=True)
            P.op("sp", lambda e: e.dma_start(out=wnm[:], in_=g_wnorm[j]), writes=[(Bk, 3)], dma=True)
            negA = sb("g_negA", [128, GH], F32, s2)
            P.op("act", lambda e: e.activation(out=negA[:], in_=alog[:], func=AF.Exp), reads=[(Bk, 1)], writes=[(Bk, 4)])
            P.op("dve", lambda e: e.tensor_scalar(out=negA[:], in0=negA[:], scalar1=-1.0, scalar2=None,
                                                  op0=ALU.mult), reads=[(Bk, 4)], writes=[(Bk, 4)])
            RK = [Bk]
            xc = sb("g_xc", [128, 8, TT + 24], F32, s2); Bxc = Buf()
            acc = sb("g_acc", [128, TT], F32, s2); Bacc = Buf()
            cs = sb("g_cs", [128, TT], F32, s2); Bcs = Buf()
            csq = sb("g_csq", [128, TT], BF16, s2); Bcsq = Buf()
            rn = sb("g_rn", [128, TT], F32, s2); Brn = Buf()
            qT = sb("g_qT", [128, GQ, TT], BF16, s2); BqT = Buf()
            kT = sb("g_kT", [128, GQ, TT], BF16, s2); BkT = Buf()
            vT = sb("g_vT", [128, GH, TT], BF16, s2); BvT = Buf()
            szT = sb("g_szT", [128, GH, TT], BF16, s2); BszT = Buf()
            ogT = [sb(f"g_ogT{i}", [128, GH, TT], BF16, s2) for i in range(2)]; BogT = [Buf(), Buf()]
            ba = sb("g_ba", [128, 8], F32, s2); Bba = Buf()
            gt = sb("g_g", [128, GH], F32, s2); Bg = Buf()
            bet = sb("g_bet", [128, GH], F32, s2); Bbet = Buf()
            nbet = sb("g_nbet", [128, GH], F32, s2)
            S = sb("g_S", [128, GH, 128], F32, s2); BS = Buf()
            Sb = sb("g_Sb", [128, GH, 128], BF16, s2); BSb = Buf()
            def T4(name, dt=F32):
                return sb(name, [128, GH, 128], dt, s2), Buf()
            gc, Bgc = sb("c_gc", [128, GH], F32, s2), Buf()
            gl, Bgl = sb("c_gl", [128, GH], F32, s2), Buf()
            edl, Bedl = sb("c_edl", [128, GH], F32, s2), Buf()
            egl, Begl = sb("c_egl", [128, GH], F32, s2), Buf()
            gb, Bgb = T4("c_gb")
            Dm, BD = T4("c_D")
            DecI, BDecI = T4("c_DecI")
            DecS, BDecS = T4("c_DecS")
            egbc, Begbc = T4("c_egbc")
            KgT, BKgT = T4("c_KgT", BF16)
            QgT, BQgT = T4("c_QgT", BF16)
            QKT, BQKT = T4("c_QKT", BF16)
            Pm = [T4("c_P0"), T4("c_P1")]
            PTm = [T4("c_PT0"), T4("c_PT1")]
            Ym = [T4("c_Y0"), T4("c_Y1")]
            XT, BXT = T4("c_XT", BF16)
            vtm, Bvtm = T4("c_vtm", BF16)
            ktm, Bktm = sb("c_ktm", [128, GQ, 128], BF16, s2), Buf()
            kdec, Bkdec = T4("c_kdec", BF16)
            rr, Brr = T4("c_r", BF16)
            vn, Bvn = T4("c_vn", BF16)
            osc, Bosc = T4("c_osc", BF16)
            junk, Bjunk = sb("c_junk", [128, 128], F32, s2), Buf()
            ss, Bss = sb("c_ss", [128, GH], F32, s2), Buf()

            def chunk(C, col0, og, Bog, ocol0):
                RQ = [BqT, BkT]
                cs_ = slice(col0, col0 + C)
                ps, bp = bank()
                for a in range(GQ):
                    P.op("pe", lambda e, a=a, ps=ps: e.matmul(ps[0:C, a * 128:(a + 1) * 128], lhsT=kT[:, a, cs_],
                                                              rhs=identb[:], start=True, stop=True),
                         reads=RQ + RC, writes=[bp])
                P.op("act", lambda e, ps=ps: e.activation(out=ktm[0:C].rearrange("p a d -> p (a d)"),
                                                          in_=ps[0:C, 0:256], func=AF.Copy), reads=[bp], writes=[Bktm])
                ps, bp = bank()
                for h in range(GH):
                    P.op("pe", lambda e, h=h, ps=ps: e.matmul(ps[0:C, h * 128:(h + 1) * 128], lhsT=vT[:, h, cs_],
                                                              rhs=identb[:], start=True, stop=True),
                         reads=[BvT] + RC, writes=[bp])
                P.op("act", lambda e, ps=ps: e.activation(out=vtm[0:C].rearrange("p a d -> p (a d)"),
                                                          in_=ps[0:C, :], func=AF.Copy), reads=[bp], writes=[Bvtm])
                ps, bp = bank()
                P.op("pe", lambda e, ps=ps: e.matmul(ps[0:C, 0:GH], lhsT=mask_i[0:C, 0:C], rhs=gt[0:C, :],
                                                     start=True, stop=True), reads=[Bg] + RC, writes=[bp])
                P.op("pe", lambda e, ps=ps: e.matmul(ps[:, 8:8 + GH], lhsT=onesf[0:C, :], rhs=gt[0:C, :],
                                                     start=True, stop=True), reads=[Bg] + RC, writes=[bp])
                P.op("dve", lambda e, ps=ps: e.tensor_copy(out=gc[0:C, :], in_=ps[0:C, 0:GH]), reads=[bp], writes=[Bgc])
                P.op("dve", lambda e, ps=ps: e.tensor_copy(out=gl[:], in_=ps[:, 8:8 + GH]), reads=[bp], writes=[Bgl])
                P.op("dve", lambda e: e.tensor_tensor(out=edl[0:C, :], in0=gl[0:C, :], in1=gc[0:C, :], op=ALU.subtract),
                     reads=[Bgl, Bgc], writes=[Bedl])
                P.op("act", lambda e: e.activation(out=edl[0:C, :], in_=edl[0:C, :], func=AF.Exp), reads=[Bedl], writes=[Bedl])
                P.op("act", lambda e: e.activation(out=egl[:], in_=gl[:], func=AF.Exp), reads=[Bgl], writes=[Begl])
                for h in range(GH):
                    P.op("pool", lambda e, h=h: e.tensor_scalar(out=gb[0:C, h, :], in0=onesf[0:C, :],
                                                               scalar1=gt[0:C, h:h + 1], scalar2=None, op0=ALU.mult),
                         reads=[Bg] + RC, writes=[(Bgb, h)])
                pB, bB = bank()
                for h in range(GH):
                    P.op("pe", lambda e, h=h, pB=pB: e.matmul(pB[:, h * 128:h * 128 + C], lhsT=gb[0:C, h, :],
                                                              rhs=mask_i[0:C, 0:C], start=True, stop=True),
                         reads=[Bgb] + RC, writes=[bB])
                for h in range(GH):
                    P.op("dve", lambda e, h=h, pB=pB: e.tensor_scalar(
                        out=Dm[0:C, h, 0:C], in0=pB[0:C, h * 128:h * 128 + C], scalar1=gc[0:C, h:h + 1], scalar2=0.0,
                        op0=ALU.subtract, op1=ALU.min), reads=[bB, Bgc], writes=[(BD, h)])
                P.op("act", lambda e: e.activation(out=Dm[0:C, :, 0:C], in_=Dm[0:C, :, 0:C], func=AF.Exp),
                     reads=[BD], writes=[BD])
                for h in range(GH):
                    P.op("pool", lambda e, h=h: e.tensor_tensor(out=DecI[0:C, h, 0:C], in0=Dm[0:C, h, 0:C],
                                                               in1=mask_i[0:C, 0:C], op=ALU.mult),
                         reads=[BD] + RC, writes=[(BDecI, h)])
                    P.op("pool", lambda e, h=h: e.tensor_tensor(out=DecS[0:C, h, 0:C], in0=Dm[0:C, h, 0:C],
                                                               in1=mask_s[0:C, 0:C], op=ALU.mult),
                         reads=[BD] + RC, writes=[(BDecS, h)])
                for h in range(GH):
                    P.op("act", lambda e, h=h, pB=pB: e.activation(out=egbc[:, h, 0:C], in_=pB[:, h * 128:h * 128 + C],
                                                                   func=AF.Exp), reads=[bB], writes=[(Begbc, h)])
                for h in range(GH):
                    P.op("dve", lambda e, h=h: e.tensor_tensor(out=KgT[:, h, 0:C], in0=kT[:, h // 2, cs_],
                                                              in1=egbc[:, h, 0:C], op=ALU.mult),
                         reads=RQ + [(Begbc, h)], writes=[(BKgT, h)])
                    P.op("pool", lambda e, h=h: e.tensor_tensor(out=QgT[:, h, 0:C], in0=qT[:, h // 2, cs_],
                                                               in1=egbc[:, h, 0:C], op=ALU.mult),
                         reads=RQ + [(Begbc, h)], writes=[(BQgT, h)])
                pC, bC = bank()
                for a in range(GQ):
                    P.op("pe", lambda e, a=a, pC=pC: e.matmul(pC[0:C, a * 128:a * 128 + C], lhsT=kT[:, a, cs_],
                                                              rhs=kT[:, a, cs_], start=True, stop=True),
                         reads=RQ, writes=[bC])
                    P.op("pe", lambda e, a=a, pC=pC: e.matmul(pC[0:C, 256 + a * 128:256 + a * 128 + C],
                                                              lhsT=kT[:, a, cs_], rhs=qT[:, a, cs_], start=True, stop=True),
                         reads=RQ, writes=[bC])
                Pc, BPc = Pm[0]
                for h in range(GH):
                    a = h // 2
                    P.op("dve", lambda e, h=h, a=a, pC=pC, Pc=Pc: e.scalar_tensor_tensor(
                        out=Pc[0:C, h, 0:C], in0=pC[0:C, a * 128:a * 128 + C], scalar=nbet[0:C, h:h + 1],
                        in1=DecS[0:C, h, 0:C], op0=ALU.mult, op1=ALU.mult), reads=[bC, Bbet, (BDecS, h)], writes=[(BPc, h)])
                    P.op("dve", lambda e, h=h, a=a, pC=pC: e.tensor_tensor(
                        out=QKT[0:C, h, 0:C], in0=pC[0:C, 256 + a * 128:256 + a * 128 + C], in1=DecI[0:C, h, 0:C],
                        op=ALU.mult), reads=[bC, (BDecI, h)], writes=[(BQKT, h)])
                pT, bT = bank()
                for h in range(GH):
                    P.op("pe", lambda e, h=h, pT=pT, Pc=Pc: e.matmul(pT[0:C, h * 128:h * 128 + C], lhsT=Pc[0:C, h, 0:C],
                                                                     rhs=ident_f[0:C, 0:C], start=True, stop=True),
                         reads=[BPc] + RC, writes=[bT])
                PTc, BPTc = PTm[0]
                P.op("act", lambda e, pT=pT, PTc=PTc: e.activation(
                    out=PTc[0:C, :, 0:C], in_=pT[0:C, :].rearrange("p (h c) -> p h c", h=GH)[:, :, 0:C], func=AF.Copy),
                    reads=[bT], writes=[BPTc])
                Yc, BYc = Ym[0]
                for h in range(GH):
                    P.op("pool", lambda e, h=h, Yc=Yc, Pc=Pc: e.tensor_tensor(out=Yc[0:C, h, 0:C], in0=Pc[0:C, h, 0:C],
                                                                             in1=ident_f[0:C, 0:C], op=ALU.add),
                         reads=[(BPc, h)] + RC, writes=[(BYc, h)])
                nst = {128: 6, 64: 5, 32: 4}[C]
                cur = 0
                v3 = lambda ps_: ps_[0:C, :].rearrange("p (h c) -> p h c", h=GH)[:, :, 0:C]
                for s_ in range(nst):
                    last = s_ == nst - 1
                    Pc, BPc = Pm[cur]; PTc, BPTc = PTm[cur]; Yc, BYc = Ym[cur]
                    Pn, BPn = Pm[1 - cur]; PTn, BPTn = PTm[1 - cur]; Yn, BYn = Ym[1 - cur]
                    p1, b1 = bank()
                    for h in range(GH):
                        P.op("pe", lambda e, h=h, p1=p1, Pc=Pc, PTc=PTc: e.matmul(
                            p1[0:C, h * 128:h * 128 + C], lhsT=Pc[0:C, h, 0:C], rhs=PTc[0:C, h, 0:C], start=True, stop=True),
                            reads=[BPc, BPTc], writes=[b1])
                    P.op("act", lambda e, p1=p1, PTn=PTn: e.activation(out=PTn[0:C, :, 0:C], in_=v3(p1), func=AF.Copy),
                         reads=[b1], writes=[BPTn])
                    if not last:
                        p2, b2 = bank()
                        for h in range(GH):
                            P.op("pe", lambda e, h=h, p2=p2, Pc=Pc, PTc=PTc: e.matmul(
                                p2[0:C, h * 128:h * 128 + C], lhsT=PTc[0:C, h, 0:C], rhs=Pc[0:C, h, 0:C], start=True, stop=True),
                                reads=[BPc, BPTc], writes=[b2])
                        P.op("dve", lambda e, p2=p2, Pn=Pn: e.tensor_copy(out=Pn[0:C, :, 0:C], in_=v3(p2)),
                             reads=[b2], writes=[BPn])
                    p3, b3 = bank()
                    for h in range(GH):
                        P.op("pe", lambda e, h=h, p3=p3, PTn=PTn, Yc=Yc: e.matmul(
                            p3[0:C, h * 128:h * 128 + C], lhsT=PTn[0:C, h, 0:C], rhs=Yc[0:C, h, 0:C], start=True, stop=True),
                            reads=[BPTn, BYc], writes=[b3])
                    if last:
                        P.op("dve", lambda e, p3=p3, Yc=Yc: e.tensor_tensor(out=XT[0:C, :, 0:C], in0=v3(p3),
                                                                           in1=Yc[0:C, :, 0:C], op=ALU.add),
                             reads=[b3, BYc], writes=[BXT])
                    else:
                        P.op("dve", lambda e, p3=p3, Yc=Yc, Yn=Yn: e.tensor_tensor(out=Yn[0:C, :, 0:C], in0=v3(p3),
                                                                                  in1=Yc[0:C, :, 0:C], op=ALU.add),
                             reads=[b3, BYc], writes=[BYn])
                    cur = 1 - cur
                for h in range(GH):
                    P.op("act", lambda e, h=h: e.activation(out=kdec[0:C, h, :], in_=ktm[0:C, h // 2, :], func=AF.Copy,
                                                            scale=edl[0:C, h:h + 1]), reads=[Bktm, Bedl], writes=[(Bkdec, h)])
                ddump("gt", gt[:], Bg); ddump("bet", bet[:], Bbet); ddump("gc", gc[:], Bgc); ddump("gl", gl[:], Bgl)
                ddump("edl", edl[:], Bedl); ddump("egl", egl[:], Begl); ddump("Dm", Dm[:], BD); ddump("DecS", DecS[:], BDecS)
                ddump("egbc", egbc[:], Begbc); ddump("XT", XT[:], BXT, BF16); ddump("QKT", QKT[:], BQKT, BF16)
                ddump("KgT", KgT[:], BKgT, BF16); ddump("kdec", kdec[:], Bkdec, BF16); ddump("vtm", vtm[:], Bvtm, BF16)
                ddump("P0", Pm[0][0][:], Pm[0][1]); ddump("P1", Pm[1][0][:], Pm[1][1]); ddump("Sb0", Sb[:], BSb, BF16)
                ddump("kT", kT[:], BkT, BF16); ddump("qT", qT[:], BqT, BF16)
                pE, bE = bank()
                for h in range(GH):
                    P.op("pe", lambda e, h=h, pE=pE: e.matmul(pE[0:C, h * 128:(h + 1) * 128], lhsT=KgT[:, h, 0:C],
                                                              rhs=Sb[:, h, :], start=True, stop=True),
                         reads=[BKgT, BSb], writes=[bE])
                P.op("dve", lambda e, pE=pE: e.tensor_tensor(out=rr[0:C].rearrange("p h d -> p (h d)"),
                                                            in0=vtm[0:C].rearrange("p h d -> p (h d)"), in1=pE[0:C, :],
                                                            op=ALU.subtract), reads=[bE, Bvtm], writes=[Brr])
                pF, bF = bank()
                for h in range(GH):
                    P.op("pe", lambda e, h=h, pF=pF: e.matmul(pF[0:C, h * 128:(h + 1) * 128], lhsT=XT[0:C, h, 0:C],
                                                              rhs=rr[0:C, h, :], start=True, stop=True),
                         reads=[BXT, Brr], writes=[bF])
                for h in range(GH):
                    P.op("act", lambda e, h=h, pF=pF: e.activation(out=vn[0:C, h, :], in_=pF[0:C, h * 128:(h + 1) * 128],
                                                                   func=AF.Copy, scale=bet[0:C, h:h + 1]),
                         reads=[bF, Bbet], writes=[(Bvn, h)])
                pG, bG = bank()
                for h in range(GH):
                    P.op("pe", lambda e, h=h, pG=pG: e.matmul(pG[0:C, h * 128:(h + 1) * 128], lhsT=QgT[:, h, 0:C],
                                                              rhs=Sb[:, h, :], start=True, stop=False),
                         reads=[BQgT, BSb], writes=[bG])
                    P.op("pe", lambda e, h=h, pG=pG: e.matmul(pG[0:C, h * 128:(h + 1) * 128], lhsT=QKT[0:C, h, 0:C],
                                                              rhs=vn[0:C, h, :], start=False, stop=True),
                         reads=[BQKT, Bvn], writes=[bG])
                pH, bH = bank()
                for h in range(GH):
                    P.op("pe", lambda e, h=h, pH=pH: e.matmul(pH[:, h * 128:(h + 1) * 128], lhsT=kdec[0:C, h, :],
                                                              rhs=vn[0:C, h, :], start=True, stop=True),
                         reads=[Bkdec, Bvn], writes=[bH])
                for h in range(GH):
                    P.op("dve", lambda e, h=h, pH=pH: e.scalar_tensor_tensor(
                        out=S[:, h, :], in0=S[:, h, :], scalar=egl[:, h:h + 1], in1=pH[:, h * 128:(h + 1) * 128],
                        op0=ALU.mult, op1=ALU.add), reads=[bH, Begl, BSb], writes=[(BS, h)])
                P.op("act", lambda e: e.activation(out=Sb[:], in_=S[:], func=AF.Copy), reads=[BS], writes=[BSb])
                for h in range(GH):
                    P.op("act", lambda e, h=h, pG=pG: e.activation(out=junk[0:C, :], in_=pG[0:C, h * 128:(h + 1) * 128],
                                                                   func=AF.Square, accum_out=ss[0:C, h:h + 1]),
                         reads=[bG], writes=[Bjunk, (Bss, h)])
                P.op("act", lambda e: e.activation(out=ss[0:C, :], in_=ss[0:C, :], func=AF.Sqrt, scale=1.0 / 128, bias=EPS),
                     reads=[Bss], writes=[Bss])
                P.op("dve", lambda e: e.reciprocal(out=ss[0:C, :], in_=ss[0:C, :]), reads=[Bss], writes=[Bss])
                for h in range(GH):
                    P.op("act", lambda e, h=h, pG=pG: e.activation(out=osc[0:C, h, :], in_=pG[0:C, h * 128:(h + 1) * 128],
                                                                   func=AF.Copy, scale=ss[0:C, h:h + 1]),
                         reads=[bG, Bss], writes=[(Bosc, h)])
                pO, bO = bank()
                for h in range(GH):
                    P.op("pe", lambda e, h=h, pO=pO: e.matmul(pO[:, h * 128:h * 128 + C], lhsT=osc[0:C, h, :],
                                                              rhs=identb[0:C, 0:C], start=True, stop=True),
                         reads=[Bosc] + RC, writes=[bO])
                for h in range(GH):
                    P.op("dve", lambda e, h=h, pO=pO: e.scalar_tensor_tensor(
                        out=og[:, h, ocol0:ocol0 + C], in0=pO[:, h * 128:h * 128 + C], scalar=wnm[:, 0:1],
                        in1=szT[:, h, cs_], op0=ALU.mult, op1=ALU.mult), reads=[bO, BszT] + RK, writes=[(Bog, h)])

            pend = nctx.run(0, src, gm[:, layer, :], add=add, bias=add_bias)
            for t in range(NTILE):
                c0 = t * TT
                sample = t >= NPT
                h, bh = pend
                if t + 1 < NTILE:
                    pend = nctx.run(t + 1, src, gm[:, layer, :], add=add, bias=add_bias)
                og, Bog = ogT[t % 2], BogT[t % 2]
                nseq = TT // DEC_S
                if t == 0:
                    P.op("pool", lambda e: e.memset(xc[:], 0.0), writes=[Bxc])
                    P.op("pool", lambda e: e.memset(S[:], 0.0), writes=[BS])
                    P.op("pool", lambda e: e.memset(Sb[:], 0.0), writes=[BSb])
                if sample:
                    b0 = (t - NPT) * nseq
                    xcv = xc[:, :, 0:nseq * 35].rearrange("p c (b w) -> p c b w", w=35)
                    P.op("sp", lambda e, b0=b0: e.dma_start(out=xcv[:, :, :, 0:3], in_=g_c0[j][:, :, b0:b0 + nseq, :]),
                         writes=[Bxc], dma=True)
                for ct in range(12):
                    ps, bp = bank()
                    for k in range(KD):
                        P.op("pe", lambda e, k=k, ct=ct, ps=ps: e.matmul(
                            ps[:, 0:TT], lhsT=w[:, k, ct * 128:(ct + 1) * 128], rhs=h[:, k, :],
                            start=(k == 0), stop=(k == KD - 1)), reads=[Bw, bh], writes=[bp])
                    if ct < 8:
                        if not sample:
                            P.op("act", lambda e, ct=ct, ps=ps: e.activation(out=xc[:, ct, 3:3 + TT], in_=ps[:, 0:TT],
                                                                             func=AF.Copy), reads=[bp], writes=[(Bxc, ct)])
                            taps = [xc[:, ct, jj:jj + TT] for jj in range(4)]
                            accv = acc[:]; csv = cs[:]
                        else:
                            P.op("act", lambda e, ct=ct, ps=ps: e.activation(
                                out=xcv[:, ct, :, 3:35], in_=ps[:, 0:TT].rearrange("p (b w) -> p b w", w=32),
                                func=AF.Copy), reads=[bp], writes=[(Bxc, ct)])
                            taps = [xcv[:, ct, :, jj:jj + 32] for jj in range(4)]
                            accv = acc[:].rearrange("p (b w) -> p b w", w=32)
                            csv = cs[:].rearrange("p (b w) -> p b w", w=32)
                        P.op("dve", lambda e, ct=ct, taps=taps, accv=accv: e.tensor_scalar(
                            out=accv, in0=taps[0], scalar1=wc[:, ct, 0:1], scalar2=None, op0=ALU.mult),
                            reads=[(Bxc, ct)] + RK, writes=[Bacc])
                        for jj in range(1, 4):
                            P.op("dve", lambda e, ct=ct, jj=jj, taps=taps, accv=accv: e.scalar_tensor_tensor(
                                out=accv, in0=taps[jj], scalar=wc[:, ct, jj:jj + 1], in1=accv, op0=ALU.mult, op1=ALU.add),
                                reads=[(Bxc, ct)] + RK, writes=[Bacc])
                        if ct < 4:
                            P.op("act", lambda e: e.activation(out=cs[:], in_=acc[:], func=AF.Silu), reads=[Bacc], writes=[Bcs])
                            P.op("act", lambda e: e.activation(out=csq[:], in_=cs[:], func=AF.Square), reads=[Bcs], writes=[Bcsq])
                            pn, bn = bank()
                            P.op("pe", lambda e, pn=pn: e.matmul(pn[:, 0:TT], lhsT=onesb[:], rhs=csq[:], start=True, stop=True),
                                 reads=[Bcsq] + RC, writes=[bn])
                            P.op("act", lambda e, pn=pn: e.activation(out=rn[:], in_=pn[:, 0:TT], func=AF.Sqrt, bias=EPS),
                                 reads=[bn], writes=[Brn])
                            P.op("dve", lambda e: e.reciprocal(out=rn[:], in_=rn[:]), reads=[Brn], writes=[Brn])
                            dst, Bd, sc = (qT, BqT, 128 ** -0.5) if ct < 2 else (kT, BkT, 1.0)
                            P.op("dve", lambda e, dst=dst, ct=ct, sc=sc: e.scalar_tensor_tensor(
                                out=dst[:, ct % 2, :], in0=cs[:], scalar=sc, in1=rn[:], op0=ALU.mult, op1=ALU.mult),
                                reads=[Bcs, Brn], writes=[(Bd, ct % 2)])
                        else:
                            P.op("act", lambda e, ct=ct: e.activation(out=vT[:, ct - 4, :], in_=acc[:], func=AF.Silu),
                                 reads=[Bacc], writes=[(BvT, ct - 4)])
                    else:
                        P.op("act", lambda e, ct=ct, ps=ps: e.activation(out=szT[:, ct - 8, :], in_=ps[:, 0:TT], func=AF.Silu),
                             reads=[bp], writes=[(BszT, ct - 8)])
                if not sample:
                    if t == NPT - 1:
                        P.op("sp", lambda e: e.dma_start(out=o_pconv[j], in_=xc[:, :, TT:TT + 3]),
                             reads=[Bxc], writes=[B_out], dma=True)
                    P.op("pool", lambda e: e.tensor_copy(out=xc[:, :, 0:3], in_=xc[:, :, TT:TT + 3]),
                         reads=[Bxc], writes=[Bxc])
                else:
                    P.op("sp", lambda e, b0=b0: e.dma_start(out=o_sconv[j][:, :, b0:b0 + nseq, :], in_=xcv[:, :, :, 32:35]),
                         reads=[Bxc], writes=[B_out], dma=True)
                CH = 128 if not sample else DEC_S
                for ci in range(TT // CH):
                    col0 = ci * CH
                    pb_, bb_ = bank()
                    for k in range(KD):
                        P.op("pe", lambda e, k=k, pb_=pb_, col0=col0, CH=CH: e.matmul(
                            pb_[0:CH, 0:8], lhsT=h[:, k, col0:col0 + CH], rhs=w[:, k, 1536:1544],
                            start=(k == 0), stop=(k == KD - 1)), reads=[Bw, bh], writes=[bb_])
                    P.op("act", lambda e, pb_=pb_, CH=CH: e.activation(out=bet[0:CH, :], in_=pb_[0:CH, 0:4], func=AF.Sigmoid),
                         reads=[bb_], writes=[Bbet])
                    P.op("dve", lambda e, CH=CH: e.tensor_scalar(out=nbet[0:CH, :], in0=bet[0:CH, :], scalar1=-1.0,
                                                                scalar2=None, op0=ALU.mult), reads=[Bbet], writes=[Bbet])
                    P.op("dve", lambda e, pb_=pb_, CH=CH: e.tensor_tensor(out=gt[0:CH, :], in0=pb_[0:CH, 4:8],
                                                                         in1=dtb[0:CH, :], op=ALU.add),
                         reads=[bb_] + RK, writes=[Bg])
                    P.op("act", lambda e, CH=CH: e.activation(out=gt[0:CH, :], in_=gt[0:CH, :], func=AF.Exp), reads=[Bg], writes=[Bg])
                    P.op("act", lambda e, CH=CH: e.activation(out=gt[0:CH, :], in_=gt[0:CH, :], func=AF.Ln, bias=1.0),
                         reads=[Bg], writes=[Bg])
                    P.op("dve", lambda e, CH=CH: e.tensor_tensor(out=gt[0:CH, :], in0=gt[0:CH, :], in1=negA[0:CH, :], op=ALU.mult),
                         reads=[Bg] + RK, writes=[Bg])
                    if sample:
                        b = (t - NPT) * nseq + ci
                        P.op("sp", lambda e, b=b: e.dma_start(out=S[:], in_=g_s0[j, b]), writes=[BS], dma=True)
                        P.op("act", lambda e: e.activation(out=Sb[:], in_=S[:], func=AF.Copy), reads=[BS], writes=[BSb])
                    dstate["on"] = (t == 0 and ci == 0 and layer == 0)
                    chunk(CH, col0, og, Bog, col0)
                    dstate["on"] = False
                    if sample:
                        P.op("sp", lambda e, b=b: e.dma_start(out=o_srec[j, b], in_=S[:]), reads=[BS], writes=[B_out], dma=True)
                    elif t == NPT - 1 and ci == TT // CH - 1:
                        P.op("sp", lambda e: e.dma_start(out=o_prec[j], in_=S[:]), reads=[BS], writes=[B_out], dma=True)
                P.op("sp", lambda e, og=og, c0=c0: e.dma_start(
                    out=o_loc.rearrange("(h p) n -> p h n", p=128)[:, :, c0:c0 + TT], in_=og[:]),
                    reads=[Bog], writes=[(B_oloc, t)], dma=True)
            P.barrier()
        allgather(o_loc, B_oloc, o_all, B_oall)
        dense_out(o_all, B_oall, 32, g_wout[j])

    def att_layer(j, layer, src, add, add_bias):
        NTB = TT // 64
        with ExitStack() as s2:
            nctx = NormCtx(s2)
            w = sb("a_w", [128, KD, 768], BF16, s2); Bw = Buf()
            load_w_bf16(w, a_wqkv[j], Bw)
            bqk = sb("a_bqk", [128, 4], F32, s2)
            bv = sb("a_bv", [64, 256], F32, s2)
            bias = sb("a_bias", [64, AH, BAND], F32, s2)
            Bk = Buf()
            P.op("sp", lambda e: e.dma_start(out=bqk[:], in_=a_bqk[j]), writes=[(Bk, 0)], dma=True)
            P.op("sp", lambda e: e.dma_start(out=bv[:], in_=a_bv[j]), writes=[(Bk, 1)], dma=True)
            P.op("sp", lambda e: e.dma_start(out=bias[:], in_=a_bias[j]), writes=[(Bk, 2)], dma=True)
            RK = [Bk]
            WK = 512 + TT
            kw = sb("a_kw", [128, AH, WK], BF16, s2); Bkw = Buf()
            vw = sb("a_vw", [64, WK // 64, AH, 128], BF16, s2); Bvw = Buf()
            qTt = sb("a_qT", [128, AH, TT], BF16, s2); BqT = Buf()
            kf = sb("a_kf", [128, AH, TT], F32, s2); Bkf = Buf()
            vf = sb("a_vf", [64, NTB, AH, 128], F32, s2); Bvf = Buf()
            Ssb = sb("a_S", [64, BAND], F32, s2); BSs = Buf()
            Eb = sb("a_E", [64, BAND], BF16, s2); BE = Buf()
            mx = sb("a_mx", [64, 1], F32, s2); Bmx = Buf()
            sm = sb("a_sm", [64, 1], F32, s2); Bsm = Buf()
            dg = sb("a_dg", [64, 64], BF16, s2); Bdg = Buf()
            PT = sb("a_PT", [64, 9, 64], BF16, s2); BPT = Buf()
            oT = [sb(f"a_oT{i}", [128, AH, TT], BF16, s2) for i in range(2)]; BoT = [Buf(), Buf()]
            P.op("pool", lambda e: e.memset(kw[:], 0.0), writes=[Bkw])
            P.op("pool", lambda e: e.memset(vw[:], 0.0), writes=[Bvw])

            NSL = 4
            slots = []
            for si_ in range(NSL):
                slots.append(dict(
                    S=sb(f"a_S{si_}", [64, BAND], F32, s2), BS=Buf(), E=sb(f"a_E{si_}", [64, BAND], BF16, s2), BE=Buf(),
                    mx=sb(f"a_mx{si_}", [64, 1], F32, s2), Bmx=Buf(), sm=sb(f"a_sm{si_}", [64, 1], F32, s2), Bsm=Buf(),
                    dg=sb(f"a_dg{si_}", [64, 64], BF16, s2), Bdg=Buf(), PT=sb(f"a_PT{si_}", [64, 9, 64], BF16, s2), BPT=Buf()))

            def attend_multi(items, ot, Bot):
                for ii, it_ in enumerate(items):
                    it_["sl"] = slots[ii]
                    it_["sb"] = []
                for it_ in items:
                    sl, Q, hd = it_["sl"], it_["Q"], it_["hd"]
                    for (kap, n) in it_["kparts"]:
                        ps, bp = bank()
                        P.op("pe", lambda e: e.matmul(ps[0:Q, 0:n], lhsT=qTt[:, hd, it_["qcol"]:it_["qcol"] + Q],
                                                      rhs=kap, start=True, stop=True), reads=[BqT, Bkw], writes=[bp])
                        it_["sb"].append((ps, bp, n))
                for it_ in items:
                    sl, Q, hd = it_["sl"], it_["Q"], it_["hd"]
                    off = 0
                    for (ps, bp, n) in it_["sb"]:
                        P.op("dve", lambda e: e.scalar_tensor_tensor(
                            out=sl["S"][0:Q, off:off + n], in0=ps[0:Q, 0:n], scalar=128 ** -0.5,
                            in1=bias[0:Q, hd, off:off + n], op0=ALU.mult, op1=ALU.add), reads=[bp] + RK, writes=[(sl["BS"], off)])
                        off += n
                    if it_["mask_upto"] > 0:
                        P.op("pool", lambda e: e.memset(sl["S"][0:Q, 0:it_["mask_upto"]], NEG), reads=[sl["BS"]], writes=[sl["BS"]])
                for it_ in items:
                    sl, Q, nk = it_["sl"], it_["Q"], it_["nkeys"]
                    P.op("dve", lambda e: e.reduce_max(out=sl["mx"][0:Q, :], in_=sl["S"][0:Q, 0:nk], axis=AX.X),
                         reads=[sl["BS"]], writes=[sl["Bmx"]])
                    P.op("dve", lambda e: e.tensor_scalar(out=sl["mx"][0:Q, :], in0=sl["mx"][0:Q, :], scalar1=-1.0, scalar2=None,
                                                          op0=ALU.mult), reads=[sl["Bmx"]], writes=[sl["Bmx"]])
                for it_ in items:
                    sl, Q, nk = it_["sl"], it_["Q"], it_["nkeys"]
                    P.op("act", lambda e: e.activation(out=sl["E"][0:Q, 0:nk], in_=sl["S"][0:Q, 0:nk], func=AF.Exp,
                                                       bias=sl["mx"][0:Q, 0:1], accum_out=sl["sm"][0:Q, 0:1]),
                         reads=[sl["BS"], sl["Bmx"]], writes=[sl["BE"], sl["Bsm"]])
                for it_ in items:
                    sl, Q = it_["sl"], it_["Q"]
                    P.op("dve", lambda e: e.reciprocal(out=sl["sm"][0:Q, :], in_=sl["sm"][0:Q, :]), reads=[sl["Bsm"]], writes=[sl["Bsm"]])
                    P.op("dve", lambda e: e.tensor_scalar(out=sl["dg"][0:Q, 0:Q], in0=identb[0:Q, 0:Q], scalar1=sl["sm"][0:Q, 0:1],
                                                          scalar2=None, op0=ALU.mult), reads=[sl["Bsm"]] + RC, writes=[sl["Bdg"]])
                for it_ in items:
                    sl, Q, nk = it_["sl"], it_["Q"], it_["nkeys"]
                    nb = (nk + 63) // 64
                    pa, ba_ = bank()
                    pb2, bb2 = bank()
                    it_["pt"] = (pa, ba_, pb2, bb2, nb)
                    for bl in range(nb):
                        kb = min(64, nk - bl * 64)
                        tgt = pa[0:kb, bl * 64:bl * 64 + Q] if bl < 8 else pb2[0:kb, 0:Q]
                        P.op("pe", lambda e: e.matmul(tgt, lhsT=sl["E"][0:Q, bl * 64:bl * 64 + kb], rhs=sl["dg"][0:Q, 0:Q],
                                                      start=True, stop=True), reads=[sl["BE"], sl["Bdg"]],
                             writes=[ba_ if bl < 8 else bb2])
                for it_ in items:
                    sl, Q, nk = it_["sl"], it_["Q"], it_["nkeys"]
                    pa, ba_, pb2, bb2, nb = it_["pt"]
                    n8 = min(nb, 8)
                    P.op("act", lambda e: e.activation(
                        out=sl["PT"][:, 0:n8, 0:Q], in_=pa[0:64, 0:n8 * 64].rearrange("p (b q) -> p b q", q=64)[:, :, 0:Q],
                        func=AF.Copy), reads=[ba_], writes=[(sl["BPT"], 0)])
                    if nb > 8:
                        kb = nk - 512
                        P.op("act", lambda e: e.activation(out=sl["PT"][0:kb, 8, 0:Q], in_=pb2[0:kb, 0:Q], func=AF.Copy),
                             reads=[bb2], writes=[(sl["BPT"], 1)])
                for it_ in items:
                    sl, Q, nk = it_["sl"], it_["Q"], it_["nkeys"]
                    nb = it_["pt"][4]
                    po, bo_ = bank()
                    it_["po"] = (po, bo_)
                    for bl in range(nb):
                        kb = min(64, nk - bl * 64)
                        P.op("pe", lambda e: e.matmul(po[:, 0:Q], lhsT=it_["vblocks"][bl][0:kb, :], rhs=sl["PT"][0:kb, bl, 0:Q],
                                                      start=(bl == 0), stop=(bl == nb - 1)), reads=[sl["BPT"], Bvw], writes=[bo_])
                for it_ in items:
                    Q, hd = it_["Q"], it_["hd"]
                    po, bo_ = it_["po"]
                    P.op("act", lambda e: e.activation(out=ot[:, hd, it_["ocol"]:it_["ocol"] + Q], in_=po[:, 0:Q], func=AF.Copy),
                         reads=[bo_], writes=[(Bot, (hd, it_["ocol"]))])

            pend = nctx.run(0, src, gm[:, layer, :], add=add, bias=add_bias)
            for t in range(NTILE):
                c0 = t * TT
                sample = t >= NPT
                h, bh = pend
                if t + 1 < NTILE:
                    pend = nctx.run(t + 1, src, gm[:, layer, :], add=add, bias=add_bias)
                ot, Bot = oT[t % 2], BoT[t % 2]
                for ct in range(4):
                    ps, bp = bank()
                    for k in range(KD):
                        P.op("pe", lambda e, k=k, ct=ct, ps=ps: e.matmul(
                            ps[:, 0:TT], lhsT=w[:, k, ct * 128:(ct + 1) * 128], rhs=h[:, k, :],
                            start=(k == 0), stop=(k == KD - 1)), reads=[Bw, bh], writes=[bp])
                    if ct < 2:
                        P.op("act", lambda e, ct=ct, ps=ps: e.activation(out=qTt[:, ct, :], in_=ps[:, 0:TT], func=AF.Identity,
                                                                         bias=bqk[:, ct:ct + 1]), reads=[bp] + RK, writes=[(BqT, ct)])
                    else:
                        P.op("act", lambda e, ct=ct, ps=ps: e.activation(out=kf[:, ct - 2, :], in_=ps[:, 0:TT], func=AF.Identity,
                                                                         bias=bqk[:, ct:ct + 1]), reads=[bp] + RK, writes=[(Bkf, ct)])
                for bl in range(NTB):
                    ps, bp = bank()
                    for k in range(KD):
                        P.op("pe", lambda e, k=k, bl=bl, ps=ps: e.matmul(
                            ps[0:64, 0:256], lhsT=h[:, k, bl * 64:(bl + 1) * 64], rhs=w[:, k, 512:768],
                            start=(k == 0), stop=(k == KD - 1)), reads=[Bw, bh], writes=[bp])
                    P.op("dve", lambda e, bl=bl, ps=ps: e.tensor_tensor(out=vf[:, bl].rearrange("p a d -> p (a d)"),
                                                                       in0=ps[0:64, 0:256], in1=bv[:], op=ALU.add),
                         reads=[bp] + RK, writes=[(Bvf, bl)])
                if not sample:
                    if t > 0:
                        P.op("pool", lambda e: e.tensor_copy(out=kw[:, :, 0:512], in_=kw[:, :, TT:TT + 512]), reads=[Bkw], writes=[Bkw])
                        P.op("pool", lambda e: e.tensor_copy(out=vw[:, 0:8], in_=vw[:, NTB:NTB + 8]), reads=[Bvw], writes=[Bvw])
                    P.op("pool", lambda e: e.tensor_copy(out=kw[:, :, 512:512 + TT], in_=kf[:]), reads=[Bkf], writes=[Bkw])
                    P.op("pool", lambda e: e.tensor_copy(out=vw[:, 8:8 + NTB], in_=vf[:]), reads=[Bvf], writes=[Bvw])
                    tl = t - (NPT - 512 // TT)
                    if tl >= 0:
                        P.op("sp", lambda e, tl=tl: e.dma_start(out=o_pk[j][:, :, tl * TT:(tl + 1) * TT], in_=kf[:]),
                             reads=[Bkf], writes=[B_out], dma=True)
                        P.op("sp", lambda e, tl=tl: e.dma_start(out=o_pv[j][:, tl * NTB:(tl + 1) * NTB], in_=vf[:]),
                             reads=[Bvf], writes=[B_out], dma=True)
                    items = []
                    for qc in range(TT // 64):
                        gchunk = t * (TT // 64) + qc
                        kstart = qc * 64
                        mask_upto = max(0, 512 - 64 * gchunk)
                        for hd in range(AH):
                            kparts = [(kw[:, hd, kstart:kstart + 512], 512), (kw[:, hd, kstart + 512:kstart + 576], 64)]
                            vbl = [vw[:, kstart // 64 + bl, hd, :] for bl in range(9)]
                            items.append(dict(hd=hd, Q=64, qcol=qc * 64, kparts=kparts, vblocks=vbl, nkeys=BAND,
                                              ocol=qc * 64, mask_upto=mask_upto))
                            if len(items) == NSL:
                                attend_multi(items, ot, Bot)
                                items = []
                else:
                    nseq = TT // DEC_S
                    P.op("sp", lambda e, c0=c0: e.dma_start(out=o_sk[j][:, :, c0 - SEQ:c0 - SEQ + TT], in_=kf[:]),
                         reads=[Bkf], writes=[B_out], dma=True)
                    for si in range(nseq):
                        b = (t - NPT) * nseq + si
                        P.op("sp", lambda e, b=b, si=si: e.dma_start(
                            out=o_sv[j, b], in_=vf[(si % 2) * 32:(si % 2) * 32 + 32, si // 2]),
                            reads=[Bvf], writes=[B_out], dma=True)
                        P.op("pool", lambda e, b=b: e.dma_start(out=kw[:, :, 0:512], in_=a_ck[j, b]), writes=[Bkw], dma=True)
                        P.op("pool", lambda e, b=b: e.dma_start(out=vw[:, 0:8], in_=a_cv[j, b]), writes=[Bvw], dma=True)
                        P.op("pool", lambda e, si=si: e.tensor_copy(out=kw[:, :, 512:544], in_=kf[:, :, si * 32:si * 32 + 32]),
                             reads=[Bkf], writes=[Bkw])
                        P.op("pool", lambda e, si=si: e.dma_start(out=vw[0:32, 8], in_=vf[(si % 2) * 32:(si % 2) * 32 + 32, si // 2]),
                             reads=[Bvf], writes=[Bvw], dma=True)
                        items = []
                        for hd in range(AH):
                            kparts = [(kw[:, hd, 0:512], 512), (kw[:, hd, 512:544], 32)]
                            vbl = [vw[:, bl, hd, :] for bl in range(9)]
                            items.append(dict(hd=hd, Q=32, qcol=si * 32, kparts=kparts, vblocks=vbl, nkeys=544,
                                              ocol=si * 32, mask_upto=0))
                        attend_multi(items, ot, Bot)
                P.op("sp", lambda e, ot=ot, c0=c0: e.dma_start(
                    out=oa_loc.rearrange("(h p) n -> p h n", p=128)[:, :, c0:c0 + TT], in_=ot[:]),
                    reads=[Bot], writes=[(B_oaloc, t)], dma=True)
            P.barrier()
        allgather(oa_loc, B_oaloc, oa_all, B_oaall)
        dense_out(oa_all, B_oaall, 16, a_wo[j])

    def dump(name, src_ap, Bsrc, dt):
        d = dout(name, list(src_ap.shape), dt)
        P.op("sp", lambda e: e.dma_start(out=d, in_=src_ap), reads=[Bsrc], writes=[B_out], dma=True)

    src = xT_in
    add = None
    addb = None
    for layer in range(depth):
        j = layer // 2
        if layer % 2 == 0:
            gdn_layer(j, layer, src, add, addb)
            if dbg and layer == 0:
                dump("dbg_o", o_all, B_oall, BF16)
                dump("dbg_y", y_all, B_yall, F32)
            ffn_layer(layer, src, None)
            if dbg and layer == 0:
                dump("dbg_x1", xT, B_xT, F32)
                dump("dbg_f", y_all, B_yall, F32)
        else:
            att_layer(j, layer, src, add, addb)
            ffn_layer(layer, xT, bo_t[:, j, :])
        src = xT
        add = y_all
        addb = None
    with ExitStack() as s2:
        nctx = NormCtx(s2)
        hf = [sb(f"fin_h{i}", [128, KD, TT], F32, s2) for i in range(2)]; Bhf = [Buf(), Buf()]
        for t in range(NTILE):
            c0 = t * TT
            h, bh = nctx.run(t, xT, gfin[:], add=y_all, h_f32=(hf[t % 2], Bhf[t % 2]))
            P.op("sp", lambda e, h=h, c0=c0: e.dma_start(out=xv(yT)[:, :, c0:c0 + TT], in_=h[:]),
                 reads=[bh], writes=[B_out], dma=True)
        P.barrier()
    P.barrier()
    P.emit()
    st.close()
    return nc


_CACHE = {}


def _host_inputs(SEQ, inp):
    f = np.float32
    NT = SEQ + DEC_B * DEC_S
    x = np.concatenate([inp["x_prompt"][0], inp["x_sample"].reshape(DEC_B * DEC_S, D)], 0)
    common = {}
    common["xT_in"] = np.ascontiguousarray(x.T).astype(f)
    jj, ii = np.meshgrid(np.arange(128), np.arange(128), indexing="ij")
    common["cst_f"] = np.ascontiguousarray(np.stack([(ii == jj), (ii >= jj), (ii > jj)], 1).astype(f))
    common["ones_f"] = np.ones((128, 128), f)
    pk = lambda a: np.ascontiguousarray(a.reshape(a.shape[0], KD, 128).transpose(2, 0, 1))
    common["gam_mix"] = pk(inp["norm_mix"])
    common["gam_ffn"] = pk(inp["norm_ffn"])
    common["gam_fin"] = np.ascontiguousarray(inp["norm_final"].reshape(KD, 128).T)
    common["a_bo"] = pk(inp["att_b_o"])
    table = inp["att_rel_bias"]
    q = np.arange(64)[:, None]
    ko = np.arange(BAND)[None, :]
    ridx = np.clip(q + 512 - ko, -256, 256) + 256
    bias_full = table[:, :, ridx]
    maps = []
    for c in range(NCORES):
        m = dict(common)
        qk = np.arange(2 * c * 128, (2 * c + 2) * 128)
        vv = np.arange(4 * c * 128, (4 * c + 4) * 128)
        hh = np.arange(4 * c, 4 * c + 4)
        cols = np.concatenate([qk, 2048 + qk, 4096 + vv, 8192 + vv, 12288 + hh, 12320 + hh])
        m["g_win"] = np.ascontiguousarray(inp["gdn_w_in"][:, :, cols])
        ch = np.concatenate([qk, 2048 + qk, 4096 + vv]).reshape(8, 128)
        m["g_wconv"] = np.ascontiguousarray(inp["gdn_w_conv"][:, :, ch].transpose(0, 3, 2, 1))
        m["g_alog"] = np.ascontiguousarray(np.broadcast_to(inp["gdn_a_log"][:, None, hh], (2, 128, GH)))
        m["g_dtb"] = np.ascontiguousarray(np.broadcast_to(inp["gdn_dt_bias"][:, None, hh], (2, 128, GH)))
        m["g_wnorm"] = np.ascontiguousarray(inp["gdn_w_norm"][:, :, None])
        m["g_wout"] = np.ascontiguousarray(inp["gdn_w_out"][:, :, 256 * c:256 * c + 256])
        m["g_s0"] = np.ascontiguousarray(inp["state_gdn_rec"][:, :, hh].transpose(0, 1, 3, 2, 4))
        m["g_c0"] = np.ascontiguousarray(inp["state_gdn_conv"][:, :, :, ch].transpose(0, 4, 3, 1, 2))
        ac = np.arange(256 * c, 256 * c + 256)
        m["a_wqkv"] = np.ascontiguousarray(inp["att_w_qkv"][:, :, np.concatenate([ac, 2048 + ac, 4096 + ac])])
        bq = inp["att_b_qkv"]
        m["a_bqk"] = np.ascontiguousarray(np.stack(
            [bq[:, 256 * c:256 * c + 128], bq[:, 256 * c + 128:256 * c + 256],
             bq[:, 2048 + 256 * c:2048 + 256 * c + 128], bq[:, 2048 + 256 * c + 128:2048 + 256 * c + 256]], 2))
        m["a_bv"] = np.ascontiguousarray(np.broadcast_to(bq[:, None, 4096 + 256 * c:4096 + 256 * c + 256], (2, 64, 256)))
        m["a_bias"] = np.ascontiguousarray(bias_full[:, 2 * c:2 * c + 2].transpose(0, 2, 1, 3))
        m["a_wo"] = np.ascontiguousarray(inp["att_w_o"][:, :, 256 * c:256 * c + 256])
        m["a_ck"] = np.ascontiguousarray(inp["cache_att_k"][:, :, :, 2 * c:2 * c + 2].transpose(0, 1, 4, 3, 2))
        cv = inp["cache_att_v"][:, :, :, 2 * c:2 * c + 2]
        m["a_cv"] = np.ascontiguousarray(cv.reshape(2, DEC_B, 8, 64, AH, 128).transpose(0, 1, 3, 2, 4, 5))
        m["f_wg"] = np.ascontiguousarray(inp["ffn_w_gate"][:, :, FFC * c:FFC * (c + 1)])
        m["f_wu"] = np.ascontiguousarray(inp["ffn_w_up"][:, :, FFC * c:FFC * (c + 1)])
        m["f_wd"] = np.ascontiguousarray(inp["ffn_w_down"][:, :, 256 * c:256 * c + 256])
        maps.append({k: np.asarray(v, dtype=f) for k, v in m.items()})
    return maps


def _assemble(SEQ, res):
    f = np.float32
    y = res[0]["yT"].T
    y_prompt = np.ascontiguousarray(y[:SEQ]).reshape(1, SEQ, D).astype(f)
    y_sample = np.ascontiguousarray(y[SEQ:]).reshape(DEC_B, DEC_S, D).astype(f)
    p_rec = np.zeros((2, 1, 32, 128, 128), f)
    p_conv = np.zeros((2, 1, 3, 8192), f)
    s_rec = np.zeros((2, DEC_B, 32, 128, 128), f)
    s_conv = np.zeros((2, DEC_B, 3, 8192), f)
    p_k = np.zeros((2, 1, 512, 16, 128), f)
    p_v = np.zeros((2, 1, 512, 16, 128), f)
    s_k = np.zeros((2, DEC_B, DEC_S, 16, 128), f)
    s_v = np.zeros((2, DEC_B, DEC_S, 16, 128), f)
    for c in range(NCORES):
        r = res[c]
        qk = np.arange(2 * c * 128, (2 * c + 2) * 128)
        vv = np.arange(4 * c * 128, (4 * c + 4) * 128)
        ch = np.concatenate([qk, 2048 + qk, 4096 + vv]).reshape(8, 128)
        p_rec[:, 0, 4 * c:4 * c + 4] = r["o_prec"].transpose(0, 2, 1, 3)
        s_rec[:, :, 4 * c:4 * c + 4] = r["o_srec"].transpose(0, 1, 3, 2, 4)
        p_conv[:, 0][:, :, ch] = r["o_pconv"].transpose(0, 3, 2, 1)
        s_conv[:, :, :, ch] = r["o_sconv"].transpose(0, 3, 4, 2, 1)
        p_k[:, 0, :, 2 * c:2 * c + 2] = r["o_pk"].transpose(0, 3, 2, 1)
        p_v[:, 0, :, 2 * c:2 * c + 2] = r["o_pv"].transpose(0, 2, 1, 3, 4).reshape(2, 512, AH, 128)
        s_k[:, :, :, 2 * c:2 * c + 2] = r["o_sk"].transpose(0, 3, 2, 1).reshape(2, DEC_B, DEC_S, AH, 128)
        s_v[:, :, :, 2 * c:2 * c + 2] = r["o_sv"]
    return (y_prompt, y_sample, p_rec, p_conv, p_k, p_v, s_rec, s_conv, s_k, s_v)


def kernel(**inputs):
    inp = {k: np.asarray(v) for k, v in inputs.items()}
    SEQ = inp["x_prompt"].shape[1]
    if SEQ not in _CACHE:
        _CACHE[SEQ] = build(SEQ)
    nc = _CACHE[SEQ]
    maps = _host_inputs(SEQ, inp)
    out = run_bass_kernel_spmd(nc, maps, core_ids=list(range(NCORES)))
    return _assemble(SEQ, out.results)


def _debug_run(inp, depth, dbg=True):
    SEQ = inp["x_prompt"].shape[1]
    nc = build(SEQ, depth=depth, dbg=dbg)
    maps = _host_inputs(SEQ, inp)
    out = run_bass_kernel_spmd(nc, maps, core_ids=list(range(NCORES)))
    return out.results, _assemble(SEQ, out.results)
```
